# Optimizing a Trainium2 kernel written in Bass

```python
import math
import jax
import jax.numpy as jnp
from jax import lax
import numpy as np

D_MODEL = 1024
BATCH = 8
SEQ = 8192
DEPTH = 4

CTX_LEN = 256
GRID_W = 64
ROPE_BASE = 10000.0
NORM_EPS = 1e-6

MLA_HEADS = 6
MLA_NOPE = 64
MLA_ROPE = 32
MLA_QK = MLA_NOPE + MLA_ROPE
MLA_V = 64
MLA_Q_RANK = 384
MLA_KV_RANK = 256
DENSE_BLOCK = 128

S5_GROUPS = 16
S5_GROUP_CH = 16
S5_CH = S5_GROUPS * S5_GROUP_CH
S5_STATE = 64

SWA_HEADS = 6
SWA_KV_HEADS = 2
SWA_HEAD_DIM = 64
SWA_WINDOW = 128
SWA_BLOCK = 128

MIX_W = MLA_HEADS * MLA_V + S5_CH + SWA_HEADS * SWA_HEAD_DIM
IN_SPLITS = (MLA_Q_RANK, MLA_KV_RANK, MLA_ROPE, S5_CH,
             SWA_HEADS * SWA_HEAD_DIM, SWA_KV_HEADS * SWA_HEAD_DIM, SWA_KV_HEADS * SWA_HEAD_DIM)
IN_W = sum(IN_SPLITS)
IN_OFFSETS = tuple(int(o) for o in np.cumsum(IN_SPLITS)[:-1])

D_FF = 2816

kernel_name = 'hybrid_mla_s5_swa_diffusion_trunk'

F32 = jnp.float32


def rms_norm(x, g):
    xf = x.astype(F32)
    y = xf * lax.rsqrt(jnp.mean(xf * xf, axis=-1, keepdims=True) + NORM_EPS)
    return (y * g.astype(F32)).astype(x.dtype)


def axial_rope_tables(rows, cols, rot_dim):
    n_freq = rot_dim // 4
    inv = ROPE_BASE ** (-jnp.arange(n_freq, dtype=F32) / n_freq)
    ang = jnp.concatenate([rows.astype(F32)[:, None] * inv[None, :],
                           cols.astype(F32)[:, None] * inv[None, :]], axis=-1)
    return jnp.cos(ang), jnp.sin(ang)


def apply_rope(x, tables):
    cos, sin = tables
    cos = cos[None, :, None, :].astype(x.dtype)
    sin = sin[None, :, None, :].astype(x.dtype)
    x1, x2 = jnp.split(x, 2, axis=-1)
    return jnp.concatenate([x1 * cos - x2 * sin, x1 * sin + x2 * cos], axis=-1)


def mla_queries(c_q, g_lora, w_uq, g_qk, rope):
    b, t, _ = c_q.shape
    q = (rms_norm(c_q, g_lora) @ w_uq).reshape(b, t, MLA_HEADS, MLA_QK)
    q = rms_norm(q, g_qk)
    if rope is not None:
        q = jnp.concatenate([q[..., :MLA_NOPE], apply_rope(q[..., MLA_NOPE:], rope)], axis=-1)
    return q


def mla_keys_values(c_kv, k_rope, g_lora, w_ukv, g_qk, rope):
    b, t, _ = c_kv.shape
    kv = (rms_norm(c_kv, g_lora) @ w_ukv).reshape(b, t, MLA_HEADS, MLA_NOPE + MLA_V)
    k_nope, v = kv[..., :MLA_NOPE], kv[..., MLA_NOPE:]
    k_r = jnp.broadcast_to(k_rope[:, :, None, :], (b, t, MLA_HEADS, MLA_ROPE))
    k = rms_norm(jnp.concatenate([k_nope, k_r], axis=-1), g_qk)
    if rope is not None:
        k = jnp.concatenate([k[..., :MLA_NOPE], apply_rope(k[..., MLA_NOPE:], rope)], axis=-1)
    return k, v


def mla_latent_attention(q, k_lat, v_lat, k_ctx, v_ctx):
    b, t, h, _ = q.shape
    keys = jnp.concatenate([k_lat, k_ctx], axis=1)
    vals = jnp.concatenate([v_lat, v_ctx], axis=1)
    n_blk = t // DENSE_BLOCK
    q_blocks = jnp.moveaxis(q.reshape(b, n_blk, DENSE_BLOCK, h, MLA_QK), 1, 0)

    def one_block(qb):
        s = jnp.einsum('bqhd,bkhd->bhqk', qb, keys).astype(F32) * (MLA_QK ** -0.5)
        p = jax.nn.softmax(s, axis=-1).astype(vals.dtype)
        return jnp.einsum('bhqk,bkhd->bqhd', p, vals)

    out = lax.map(one_block, q_blocks)
    return jnp.moveaxis(out, 0, 1).reshape(b, t, h * MLA_V)


def mla_context_attention(q, k, v):
    b, t, h, _ = q.shape
    s = jnp.einsum('bqhd,bkhd->bhqk', q, k).astype(F32) * (MLA_QK ** -0.5)
    p = jax.nn.softmax(s, axis=-1).astype(v.dtype)
    return jnp.einsum('bhqk,bkhd->bqhd', p, v).reshape(b, t, h * MLA_V)


def s5_discretize(a_re, a_im, log_dt, b_re, b_im):
    lam = lax.complex(a_re.astype(F32), a_im.astype(F32))
    dt = jnp.exp(log_dt.astype(F32))[:, None]
    a_bar = jnp.exp(lam * dt)
    b_mat = lax.complex(b_re.astype(F32), b_im.astype(F32))
    b_bar = ((a_bar - 1.0) / lam)[:, :, None] * b_mat
    return a_bar, b_bar


def _linear_recurrence_combine(e1, e2):
    a1, b1 = e1
    a2, b2 = e2
    return a1 * a2, a2[:, None] * b1 + b2


def s5_scan(u, a_bar, b_bar, h0, reverse):
    bu = jnp.einsum('btgh,gph->tbgp', u.astype(jnp.complex64), b_bar)
    if h0 is not None:
        bu = bu.at[-1 if reverse else 0].add(a_bar * h0)
    a = jnp.broadcast_to(a_bar, (bu.shape[0],) + a_bar.shape)
    _, h = lax.associative_scan(_linear_recurrence_combine, (a, bu), reverse=reverse, axis=0)
    return h


def s5_readout(h, c_mat):
    return jnp.real(jnp.einsum('tbgp,ghp->btgh', h, c_mat))


def s5_bidirectional(u_lat, u_ctx, a_re, a_im, log_dt, b_re, b_im, c_re, c_im, d_skip, w_glu, b_glu,
                     need_ctx):
    b, t, _ = u_lat.shape
    n_ctx = u_ctx.shape[1]
    ul = u_lat.astype(F32)
    uc = u_ctx.astype(F32)
    ul_g = ul.reshape(b, t, S5_GROUPS, S5_GROUP_CH)
    uc_g = uc.reshape(b, n_ctx, S5_GROUPS, S5_GROUP_CH)
    d32 = d_skip.astype(F32)
    y_lat = d32 * ul
    y_ctx = d32 * uc if need_ctx else None
    for direction in range(2):
        reverse = direction == 1
        a_bar, b_bar = s5_discretize(a_re[direction], a_im[direction], log_dt[direction],
                                     b_re[direction], b_im[direction])
        c_mat = lax.complex(c_re[direction].astype(F32), c_im[direction].astype(F32))
        h_ctx = s5_scan(uc_g, a_bar, b_bar, None, reverse)
        h0 = h_ctx[0] if reverse else h_ctx[-1]
        h_lat = s5_scan(ul_g, a_bar, b_bar, h0, reverse)
        y_lat = y_lat + s5_readout(h_lat, c_mat).reshape(b, t, S5_CH)
        if need_ctx:
            y_ctx = y_ctx + s5_readout(h_ctx, c_mat).reshape(b, n_ctx, S5_CH)

    def half_glu(y):
        z = jax.nn.gelu(y)
        return z * jax.nn.sigmoid(z @ w_glu.astype(F32) + b_glu.astype(F32))

    out_lat = half_glu(y_lat).astype(u_lat.dtype)
    out_ctx = half_glu(y_ctx).astype(u_ctx.dtype) if need_ctx else None
    return out_lat, out_ctx


def swa_queries(q_raw, g_qk, rope):
    b, t, _ = q_raw.shape
    q = rms_norm(q_raw.reshape(b, t, SWA_HEADS, SWA_HEAD_DIM), g_qk)
    return apply_rope(q, rope) if rope is not None else q


def swa_keys_values(k_raw, v_raw, g_qk, rope):
    b, t, _ = k_raw.shape
    k = rms_norm(k_raw.reshape(b, t, SWA_KV_HEADS, SWA_HEAD_DIM), g_qk)
    if rope is not None:
        k = apply_rope(k, rope)
    return k, v_raw.reshape(b, t, SWA_KV_HEADS, SWA_HEAD_DIM)


def swa_latent_attention(q, k, v, k_ctx, v_ctx, sink):
    b, t, h, d = q.shape
    blk = SWA_BLOCK
    n_blk = t // blk
    rep = h // SWA_KV_HEADS
    n_ctx = k_ctx.shape[1]
    qb = q.reshape(b, n_blk, blk, SWA_KV_HEADS, rep, d)

    def band(z):
        zp = jnp.pad(z, ((0, 0), (blk, blk), (0, 0), (0, 0))).reshape(b, n_blk + 2, blk, SWA_KV_HEADS, d)
        return jnp.concatenate([zp[:, :-2], zp[:, 1:-1], zp[:, 2:]], axis=2)

    kb, vb = band(k), band(v)
    q_pos = jnp.arange(n_blk)[:, None] * blk + jnp.arange(blk)[None, :]
    k_pos = jnp.arange(n_blk)[:, None] * blk - blk + jnp.arange(3 * blk)[None, :]
    rel = k_pos[:, None, :] - q_pos[:, :, None]
    valid = (jnp.abs(rel) <= SWA_WINDOW) & (k_pos[:, None, :] >= 0) & (k_pos[:, None, :] < t)
    scale = d ** -0.5
    s_band = jnp.einsum('bnqgrd,bnkgd->bngrqk', qb, kb).astype(F32) * scale
    s_band = jnp.where(valid[None, :, None, None], s_band, -jnp.inf)
    s_ctx = jnp.einsum('bnqgrd,blgd->bngrql', qb, k_ctx).astype(F32) * scale
    s_sink = jnp.broadcast_to(sink.astype(F32).reshape(SWA_KV_HEADS, rep)[None, None, :, :, None, None],
                              s_ctx.shape[:-1] + (1,))
    p = jax.nn.softmax(jnp.concatenate([s_sink, s_ctx, s_band], axis=-1), axis=-1).astype(v.dtype)
    out = (jnp.einsum('bngrql,blgd->bnqgrd', p[..., 1:1 + n_ctx], v_ctx)
           + jnp.einsum('bngrqk,bnkgd->bnqgrd', p[..., 1 + n_ctx:], vb))
    return out.reshape(b, t, h * d)


def swa_context_attention(q, k, v, sink):
    b, t, h, d = q.shape
    rep = h // SWA_KV_HEADS
    qg = q.reshape(b, t, SWA_KV_HEADS, rep, d)
    s = jnp.einsum('bqgrd,bkgd->bgrqk', qg, k).astype(F32) * (d ** -0.5)
    s_sink = jnp.broadcast_to(sink.astype(F32).reshape(SWA_KV_HEADS, rep)[None, :, :, None, None],
                              s.shape[:-1] + (1,))
    p = jax.nn.softmax(jnp.concatenate([s_sink, s], axis=-1), axis=-1)[..., 1:].astype(v.dtype)
    return jnp.einsum('bgrqk,bkgd->bqgrd', p, v).reshape(b, t, h * d)


def conv_ffn(h, w_up, conv_w, conv_b, w_down):
    u = h @ w_up
    up = jnp.pad(u, ((0, 0), (1, 1), (0, 0)))
    u = up[:, :-2] * conv_w[0] + up[:, 1:-1] * conv_w[1] + up[:, 2:] * conv_w[2] + conv_b
    a, g = jnp.split(u, 2, axis=-1)
    return (jax.nn.silu(g) * a) @ w_down


def setup_inputs(seed: int = 0) -> dict:
    key = jax.random.key(seed)
    keys = iter(jax.random.split(key, 48))

    def nrm(shape, scale):
        return scale * jax.random.normal(next(keys), shape, F32)

    L, D = DEPTH, D_MODEL
    G, P, Hg = S5_GROUPS, S5_STATE, S5_GROUP_CH
    state_idx = jnp.arange(P, dtype=F32)
    return {
        'x': nrm((BATCH, SEQ, D), 1.0),
        'c': nrm((BATCH, D), 1.0),
        'ctx': nrm((BATCH, CTX_LEN, D), 1.0),
        'c_ctx': nrm((D,), 1.0),
        'w_mod': nrm((L, D, 6 * D), 0.5 * D ** -0.5),
        'b_mod': nrm((L, 6 * D), 0.01),
        'norm1': 1.0 + nrm((L, D), 0.1),
        'w_in': nrm((L, D, IN_W), D ** -0.5),
        'mla_q_lora_g': 1.0 + nrm((L, MLA_Q_RANK), 0.1),
        'mla_w_uq': nrm((L, MLA_Q_RANK, MLA_HEADS * MLA_QK), MLA_Q_RANK ** -0.5),
        'mla_kv_lora_g': 1.0 + nrm((L, MLA_KV_RANK), 0.1),
        'mla_w_ukv': nrm((L, MLA_KV_RANK, MLA_HEADS * (MLA_NOPE + MLA_V)), MLA_KV_RANK ** -0.5),
        'mla_q_norm': 1.0 + nrm((L, MLA_QK), 0.1),
        'mla_k_norm': 1.0 + nrm((L, MLA_QK), 0.1),
        's5_a_re': -0.5 + nrm((L, 2, G, P), 0.01),
        's5_a_im': math.pi * state_idx + nrm((L, 2, G, P), 0.01),
        's5_log_dt': jax.random.uniform(next(keys), (L, 2, G), F32, math.log(1e-3), math.log(1e-1)),
        's5_b_re': nrm((L, 2, G, P, Hg), (2 * Hg) ** -0.5),
        's5_b_im': nrm((L, 2, G, P, Hg), (2 * Hg) ** -0.5),
        's5_c_re': nrm((L, 2, G, Hg, P), P ** -0.5),
        's5_c_im': nrm((L, 2, G, Hg, P), P ** -0.5),
        's5_d': nrm((L, S5_CH), 1.0),
        's5_w_glu': nrm((L, S5_CH, S5_CH), S5_CH ** -0.5),
        's5_b_glu': nrm((L, S5_CH), 0.01),
        'swa_q_norm': 1.0 + nrm((L, SWA_HEAD_DIM), 0.1),
        'swa_k_norm': 1.0 + nrm((L, SWA_HEAD_DIM), 0.1),
        'swa_sink': nrm((L, SWA_HEADS), 0.5),
        'w_out': nrm((L, MIX_W, D), MIX_W ** -0.5),
        'norm2': 1.0 + nrm((L, D), 0.1),
        'w_up': nrm((L, D, 2 * D_FF), D ** -0.5),
        'conv_w': jnp.array([0.0, 1.0, 0.0], F32)[None, :, None] + nrm((L, 3, 2 * D_FF), 0.2),
        'conv_b': nrm((L, 2 * D_FF), 0.01),
        'w_down': nrm((L, D_FF, D), D_FF ** -0.5),
    }


def reference(x, c, ctx, c_ctx, w_mod, b_mod, norm1, w_in, mla_q_lora_g, mla_w_uq, mla_kv_lora_g,
              mla_w_ukv, mla_q_norm, mla_k_norm, s5_a_re, s5_a_im, s5_log_dt, s5_b_re, s5_b_im,
              s5_c_re, s5_c_im, s5_d, s5_w_glu, s5_b_glu, swa_q_norm, swa_k_norm, swa_sink, w_out,
              norm2, w_up, conv_w, conv_b, w_down):
    n_lat = x.shape[1]
    ROWS = n_lat // GRID_W
    rows = jnp.repeat(jnp.arange(ROWS), GRID_W)
    cols = jnp.tile(jnp.arange(GRID_W), ROWS)
    rope_mla = axial_rope_tables(rows, cols, MLA_ROPE)
    rope_swa = axial_rope_tables(rows, cols, SWA_HEAD_DIM)

    s_lat = jax.nn.silu(c)[:, None, :]
    s_ctx = jax.nn.silu(c_ctx)
    x_lat, x_ctx = x, ctx
    for l in range(DEPTH):
        need_ctx = l < DEPTH - 1
        sh1, sc1, g1, sh2, sc2, g2 = jnp.split(s_lat @ w_mod[l] + b_mod[l], 6, axis=-1)
        csh1, csc1, cg1, csh2, csc2, cg2 = jnp.split(s_ctx @ w_mod[l] + b_mod[l], 6, axis=-1)

        h_lat = rms_norm(x_lat, norm1[l]) * (1.0 + sc1) + sh1
        h_ctx = rms_norm(x_ctx, norm1[l]) * (1.0 + csc1) + csh1
        cq_l, ckv_l, kr_l, u_l, qs_l, ks_l, vs_l = jnp.split(h_lat @ w_in[l], IN_OFFSETS, axis=-1)
        cq_c, ckv_c, kr_c, u_c, qs_c, ks_c, vs_c = jnp.split(h_ctx @ w_in[l], IN_OFFSETS, axis=-1)

        qa = mla_queries(cq_l, mla_q_lora_g[l], mla_w_uq[l], mla_q_norm[l], rope_mla)
        ka, va = mla_keys_values(ckv_l, kr_l, mla_kv_lora_g[l], mla_w_ukv[l], mla_k_norm[l], rope_mla)
        ka_c, va_c = mla_keys_values(ckv_c, kr_c, mla_kv_lora_g[l], mla_w_ukv[l], mla_k_norm[l], None)
        ya = mla_latent_attention(qa, ka, va, ka_c, va_c)

        yb, yb_c = s5_bidirectional(u_l, u_c, s5_a_re[l], s5_a_im[l], s5_log_dt[l], s5_b_re[l], s5_b_im[l],
                                    s5_c_re[l], s5_c_im[l], s5_d[l], s5_w_glu[l], s5_b_glu[l], need_ctx)

        qs = swa_queries(qs_l, swa_q_norm[l], rope_swa)
        ks, vs = swa_keys_values(ks_l, vs_l, swa_k_norm[l], rope_swa)
        ks_cc, vs_cc = swa_keys_values(ks_c, vs_c, swa_k_norm[l], None)
        yc = swa_latent_attention(qs, ks, vs, ks_cc, vs_cc, swa_sink[l])

        x_lat = x_lat + g1 * (jnp.concatenate([ya, yb, yc], axis=-1) @ w_out[l])
        if need_ctx:
            qa_c = mla_queries(cq_c, mla_q_lora_g[l], mla_w_uq[l], mla_q_norm[l], None)
            ya_c = mla_context_attention(qa_c, ka_c, va_c)
            qs_cc = swa_queries(qs_c, swa_q_norm[l], None)
            yc_c = swa_context_attention(qs_cc, ks_cc, vs_cc, swa_sink[l])
            x_ctx = x_ctx + cg1 * (jnp.concatenate([ya_c, yb_c, yc_c], axis=-1) @ w_out[l])

        h_lat = rms_norm(x_lat, norm2[l]) * (1.0 + sc2) + sh2
        x_lat = x_lat + g2 * conv_ffn(h_lat, w_up[l], conv_w[l], conv_b[l], w_down[l])
        if need_ctx:
            h_ctx = rms_norm(x_ctx, norm2[l]) * (1.0 + csc2) + csh2
            x_ctx = x_ctx + cg2 * conv_ffn(h_ctx, w_up[l], conv_w[l], conv_b[l], w_down[l])
    return x_lat
```

```python
import math
import numpy as np
import concourse.bass as bass
import concourse.mybir as mybir
from concourse.bass_utils import run_bass_kernel_spmd

F32 = mybir.dt.float32
BF16 = mybir.dt.bfloat16
I32 = mybir.dt.int32
AF = mybir.ActivationFunctionType
ALU = mybir.AluOpType

D = 1024
TC = 256
EPS = 1e-6
MAGIC = 1.5 * 2 ** 23
TWO_PI = 2.0 * math.pi
IN_OFF = dict(cq=0, ckv=384, kr=640, u=672, qs=928, ks=1312, vs=1440)
DFF = 2816


class Buf:
    __slots__ = ("w", "r", "psum")

    def __init__(self, psum=False):
        self.w = None
        self.r = {}
        self.psum = psum


class FW:
    def __init__(self, nc, n_dma_sems=40):
        self.nc = nc
        self.engs = {"pe": nc.tensor, "act": nc.scalar, "dve": nc.vector, "pool": nc.gpsimd, "sp": nc.sync}
        self.sem, self.cnt = {}, {}
        self.seen = {k: {} for k in self.engs}
        self._ctx = []
        for k in ("pe", "act", "dve", "pool"):
            g = nc.semaphore("s_" + k)
            self.sem[k] = g.__enter__()
            self._ctx.append(g)
            self.cnt[k] = 0
        self.dsems = []
        for i in range(n_dma_sems):
            g = nc.semaphore("d_%d" % i)
            self.dsems.append(g.__enter__())
            self._ctx.append(g)
        self.dcount = [0] * n_dma_sems
        self.dnext = 0
        self.bufs = {}
        self.n_wait = 0
        self.n_inst = 0
        import os
        self.lim = int(os.environ.get("KLIM", "1000000000"))

    def close(self):
        for g in reversed(self._ctx):
            g.__exit__(None, None, None)

    def _key(self, x):
        if isinstance(x, (str, tuple)):
            k = x
        elif hasattr(x, "tensor"):
            k = x.tensor.name
        else:
            k = x.name
        b = self.bufs.get(k)
        if b is None:
            b = self.bufs[k] = Buf(isinstance(k, str) and k.startswith("pb"))
        return b

    def _wait(self, e, dep, same_ok=False):
        if dep is None:
            return
        kind, s, v = dep
        if kind == "eng" and s == e and (not same_ok or e == "pe"):
            return
        seen = self.seen[e]
        key = (kind, s)
        if seen.get(key, -1) >= v:
            return
        seen[key] = v
        self.engs[e].wait_ge(self.sem[s] if kind == "eng" else self.dsems[s], v)
        self.n_wait += 1

    def _deps(self, e, reads, writes):
        rb = [self._key(x) for x in reads]
        wb = [self._key(x) for x in writes]
        for b in rb:
            self._wait(e, b.w, True)
            if b.psum:
                for d in b.r.values():
                    self._wait(e, d)
        for b in wb:
            self._wait(e, b.w, True)
            for d in b.r.values():
                self._wait(e, d)
        return rb, wb

    def _commit(self, dep, rb, wb, rkey):
        for b in rb:
            b.r[rkey] = dep
        for b in wb:
            b.w = dep
            b.r = {}

    def op(self, e, fn, reads, writes):
        if self.n_inst >= self.lim:
            return
        rb, wb = self._deps(e, reads, writes)
        ins = fn()
        self.cnt[e] += 1
        ins.then_inc(self.sem[e], 1)
        self.n_inst += 1
        self._commit(("eng", e, self.cnt[e]), rb, wb, e)

    def dma(self, e, out, in_, reads, writes, **kw):
        if self.n_inst >= self.lim:
            return
        rb, wb = self._deps(e, reads, writes)
        i = self.dnext
        self.dnext = (self.dnext + 1) % len(self.dsems)
        if self.dcount[i]:
            self._wait(e, ("dma", i, self.dcount[i]))
        self.dcount[i] += 16
        self.engs[e].dma_start(out=out, in_=in_, **kw).then_inc(self.dsems[i], 16)
        self.n_inst += 1
        self._commit(("dma", i, self.dcount[i]), rb, wb, ("dma", i))

    def barrier(self):
        for e in self.engs:
            for o in ("pe", "act", "dve", "pool"):
                if o != e:
                    self._wait(e, ("eng", o, self.cnt[o]))
            for i in range(len(self.dsems)):
                if self.dcount[i]:
                    self._wait(e, ("dma", i, self.dcount[i]))
        self.bufs = {}


class Scope:
    def __init__(self, kb):
        self.kb = kb
        self.gs = []

    def sb(self, name, shape, dt=F32):
        self.kb.uid += 1
        g = self.kb.nc.sbuf_tensor("%s_%d" % (name, self.kb.uid), list(shape), dt)
        t = g.__enter__()
        self.gs.append(g)
        return t

    def close(self):
        self.kb.fw.barrier()
        for g in reversed(self.gs):
            g.__exit__(None, None, None)
        self.gs = []


class KB:
    def __init__(self, T, depth, dbg=(), l0=0):
        self.T, self.NT, self.depth, self.dbg = T, TC + T, depth, set(dbg)
        self.l0 = l0
        self.NTT = self.NT // 128
        nc = self.nc = bass.Bass("TRN2", target_bir_lowering=False)
        self.fw = FW(nc)
        self.uid = 0
        self.inp = {}
        self.glob = Scope(self)
        self.pb = []
        for i in range(8):
            g = nc.psum_tensor("pb%d" % i, [128, 512], F32)
            self.pb.append(g.__enter__())
            self.fw._ctx.append(g)
        self.chunks = [(0, TC)] + [(TC + 512 * i, 512) for i in range(T // 512)]

    def din(self, name, shape, dt=F32):
        a = self.nc.dram_tensor(name, list(shape), dt, kind="ExternalInput").ap()
        self.inp[name] = a
        return a

    def dscr(self, name, shape, dt=F32):
        kind = "ExternalOutput" if name in self.dbg else "Internal"
        return self.nc.dram_tensor(name, list(shape), dt, kind=kind).ap()

    def _rw(self, outs, ins, R, W):
        r = list(R) if R is not None else [x for x in ins if not isinstance(x, (int, float)) and x is not None]
        w = list(W) if W is not None else list(outs)
        return r, w

    def mm(self, out, lhsT, rhs, start=True, stop=True, R=None, W=None):
        r, w = self._rw([out], [lhsT, rhs], R, W)
        self.fw.op("pe", lambda: self.nc.tensor.matmul(out, lhsT=lhsT, rhs=rhs, start=start, stop=stop), r, w)

    def tr(self, out, in_, ident, R=None, W=None):
        r, w = self._rw([out], [in_, ident], R, W)
        self.fw.op("pe", lambda: self.nc.tensor.transpose(out, in_, ident), r, w)

    def act(self, out, in_, func, scale=1.0, bias=None, R=None, W=None):
        r, w = self._rw([out], [in_, scale, bias], R, W)
        kw = {}
        if bias is not None:
            kw["bias"] = bias
        self.fw.op("act", lambda: self.nc.scalar.activation(out=out, in_=in_, func=func, scale=scale, **kw), r, w)

    def ts(self, e, out, in0, s1, s2=None, op0=ALU.mult, op1=None, R=None, W=None):
        r, w = self._rw([out], [in0, s1, s2], R, W)
        eng = self.fw.engs[e]
        if op1 is None:
            f = lambda: eng.tensor_scalar(out=out, in0=in0, scalar1=s1, scalar2=None, op0=op0)
        else:
            f = lambda: eng.tensor_scalar(out=out, in0=in0, scalar1=s1, scalar2=s2, op0=op0, op1=op1)
        self.fw.op(e, f, r, w)

    def tt(self, e, out, in0, in1, op, R=None, W=None):
        r, w = self._rw([out], [in0, in1], R, W)
        eng = self.fw.engs[e]
        self.fw.op(e, lambda: eng.tensor_tensor(out=out, in0=in0, in1=in1, op=op), r, w)

    def stt(self, out, in0, scalar, in1, op0, op1, R=None, W=None):
        r, w = self._rw([out], [in0, scalar, in1], R, W)
        self.fw.op("dve", lambda: self.nc.vector.scalar_tensor_tensor(out=out, in0=in0, scalar=scalar, in1=in1, op0=op0, op1=op1), r, w)

    def cp(self, e, out, in_, R=None, W=None):
        r, w = self._rw([out], [in_], R, W)
        if e == "act":
            f = lambda: self.nc.scalar.copy(out=out, in_=in_)
        else:
            eng = self.fw.engs[e]
            f = lambda: eng.tensor_copy(out=out, in_=in_)
        self.fw.op(e, f, r, w)

    def recip(self, out, in_, R=None, W=None):
        r, w = self._rw([out], [in_], R, W)
        self.fw.op("dve", lambda: self.nc.vector.reciprocal(out=out, in_=in_), r, w)

    def memset(self, e, out, val, R=None, W=None):
        r, w = self._rw([out], [], R, W)
        eng = self.fw.engs[e]
        self.fw.op(e, lambda: eng.memset(out, val), r, w)

    def scan(self, out, d0, d1, init, R=None, W=None):
        r, w = self._rw([out], [d0, d1, init], R, W)
        self.fw.op("dve", lambda: self.nc.vector.tensor_tensor_scan(out=out, data0=d0, data1=d1, initial=init, op0=ALU.mult, op1=ALU.add), r, w)

    def ld(self, out, in_, R=None, W=None, e="sp", **kw):
        r, w = self._rw([out], [in_], R, W)
        self.fw.dma(e, out, in_, r, w, **kw)

    def ld_T(self, sc, dst, src_rows, n, pbank):
        stg = sc.sb("ldT", [128, 128])
        self.ld(stg[0:n, :], src_rows)
        self.tr(pbank[:, 0:n], stg[0:n, :], self.identF[0:n, 0:n])
        self.cp("dve", dst, pbank[:, 0:n])

    def rstd(self, out, in_, scale, bias_ap):
        self.act(out, in_, AF.Sqrt, scale=scale, bias=bias_ap)
        self.recip(out, out)

    def build(self):
        nc, T, NT = self.nc, self.T, self.NT
        L = 4
        i_ = self.din
        self.x_d = i_("x", [T, D])
        self.ctx_d = i_("ctx", [TC, D])
        self.cvec_d = i_("cvec", [2, D])
        self.ropeab_d = i_("rope_ab", [128, 4])
        self.w = {}
        G, PS, HG = 16, 64, 16
        shapes = dict(w_mod=[L, D, 6 * D], b_mod=[L, 6 * D], norm1=[L, D], w_in=[L, D, 1568], mla_q_lora_g=[L, 384],
                      mla_w_uq=[L, 384, 576], mla_kv_lora_g=[L, 256], mla_w_ukv=[L, 256, 768], mla_q_norm=[L, 96],
                      mla_k_norm=[L, 96], s5_a_re=[L, 2, G, PS], s5_a_im=[L, 2, G, PS], s5_log_dt=[L, 2, G],
                      s5_b_re=[L, 2, G, PS, HG], s5_b_im=[L, 2, G, PS, HG], s5_c_re=[L, 2, G, HG, PS],
                      s5_c_im=[L, 2, G, HG, PS], s5_d=[L, 256], s5_w_glu=[L, 256, 256], s5_b_glu=[L, 256],
                      swa_q_norm=[L, 64], swa_k_norm=[L, 64], swa_sink=[L, 6], w_out=[L, D, D], norm2=[L, D],
                      w_up=[L, D, 2 * DFF], conv_w=[L, 3, 2 * DFF], conv_b=[L, 2 * DFF], w_down=[L, DFF, D])
        for k, s in shapes.items():
            self.w[k] = i_(k, s)
        self.out_d = nc.dram_tensor("out", [T, D], F32, kind="ExternalOutput").ap()
        s_ = self.dscr
        self.xT = [s_("xTa", [D, NT]), s_("xTb", [D, NT])]
        self.qT = s_("qT", [6, 96, NT], BF16)
        self.kT = s_("kT", [6, 96, NT], BF16)
        self.vA = s_("vA", [6, 128, NT // 128, 64], BF16)
        self.qsT = s_("qsT", [6, 64, NT], BF16)
        self.ksT = s_("ksT", [128, NT], BF16)
        self.vS = s_("vS", [2, 128, NT // 128, 64], BF16)
        self.uTb = s_("uTb", [256, NT], BF16)
        self.uT32 = s_("uT32", [256, NT])
        self.ysum = s_("ysum", [256, NT])
        self.ycT = s_("ycT", [D, NT], BF16)
        self.ropeM = [s_("ropeMC", [96, NT]), s_("ropeMS", [96, NT])]
        self.ropeS = [s_("ropeSC", [128, NT]), s_("ropeSS", [128, NT])]
        self._ncd = nc.allow_non_contiguous_dma(reason="small strided parameter loads")
        self._ncd.__enter__()
        self.phase0()
        cur = 0
        for l in range(self.l0, self.l0 + self.depth):
            last = l == self.l0 + self.depth - 1
            if "stop0" in self.dbg:
                break
            self.phaseM(l)
            if "stopM" in self.dbg:
                break
            self.phaseA(l, self.xT[cur])
            if "stopA" in self.dbg:
                break
            self.phaseB(l, last and "stopC" not in self.dbg)
            self.phaseC(l, last and "stopC" not in self.dbg)
            if "stopC" in self.dbg and "stopD" not in self.dbg:
                break
            self.phaseD(l, last)
            if "stopD" in self.dbg:
                break
            self.phaseE(l, self.xT[cur], self.xT[1 - cur], last)
            if "stopL1" in self.dbg:
                break
        if not ({"stopA", "stop0", "stopM", "stopC", "stopD", "stopL1"} & self.dbg):
            self.phaseOut(self.xT[cur])
        self.fw.barrier()
        self._ncd.__exit__(None, None, None)
        return nc

    def phase0(self):
        nc, T, NT = self.nc, self.T, self.NT
        g = self.glob
        self.identF = g.sb("identF", [128, 128])
        self.onesB = g.sb("onesB", [128, 128], BF16)
        self.onesBD = g.sb("onesBD", [128, 128], BF16)
        self.mlo = g.sb("mlo", [128, 128], BF16)
        self.mhi = g.sb("mhi", [128, 128], BF16)
        self.cst = g.sb("cst", [128, 8])
        for j, v in enumerate([EPS, 96 * EPS, 64 * EPS, math.pi / 2, 0.0]):
            self.memset("dve", self.cst[:, j:j + 1], v)
        self.memset("pool", self.identF[:], 1.0)
        self.fw.op("pool", lambda: nc.gpsimd.affine_select(out=self.identF[:], in_=self.identF[:], compare_op=ALU.is_equal,
                                                           fill=0.0, base=0, pattern=[[-1, 128]], channel_multiplier=1),
                   [self.identF], [self.identF])
        self.memset("dve", self.onesB[:], 1.0)
        self.memset("dve", self.onesBD[:], 0.0)
        self.memset("dve", self.onesBD[0:64, 0:64], 1.0)
        self.memset("dve", self.onesBD[64:128, 64:128], 1.0)
        self.memset("pool", self.mlo[:], 1.0)
        self.memset("pool", self.mhi[:], 1.0)
        self.fw.op("pool", lambda: nc.gpsimd.affine_select(out=self.mlo[:], in_=self.mlo[:], compare_op=ALU.is_ge, fill=0.0,
                                                           base=0, pattern=[[-1, 128]], channel_multiplier=1),
                   [self.mlo], [self.mlo])
        self.fw.op("pool", lambda: nc.gpsimd.affine_select(out=self.mhi[:], in_=self.mhi[:], compare_op=ALU.is_ge, fill=0.0,
                                                           base=0, pattern=[[1, 128]], channel_multiplier=-1),
                   [self.mhi], [self.mhi])
        sc = Scope(self)
        ab = sc.sb("ab", [128, 4])
        self.ld(ab[:], self.ropeab_d)
        RC = min(2048, T)
        rowi = sc.sb("rowi", [128, RC], I32)
        coli = sc.sb("coli", [128, RC], I32)
        rowf = sc.sb("rowf", [128, RC])
        colf = sc.sb("colf", [128, RC])
        self.fw.op("pool", lambda: nc.gpsimd.iota(rowi[:], pattern=[[1, RC // 64], [0, 64]], base=0, channel_multiplier=0), [], [rowi])
        self.fw.op("pool", lambda: nc.gpsimd.iota(coli[:], pattern=[[0, RC // 64], [1, 64]], base=0, channel_multiplier=0), [], [coli])
        self.cp("dve", colf[:], coli[:])
        rowf0 = sc.sb("rowf0", [128, RC])
        self.cp("dve", rowf0[:], rowi[:])
        one = sc.sb("one", [128, TC])
        zero = sc.sb("zero", [128, TC])
        self.memset("dve", one[:], 1.0)
        self.memset("dve", zero[:], 0.0)
        ang = sc.sb("ang", [128, RC])
        tmp = sc.sb("tmpa", [128, RC])
        cs = sc.sb("cs", [128, RC])
        sn = sc.sb("sn", [128, RC])
        for (tabs, npart, ca, cb, nonrope) in ((self.ropeM, 96, 0, 1, 64), (self.ropeS, 128, 2, 3, 0)):
            self.ld(tabs[0][:, 0:TC], one[0:npart, :], W=[tabs[0].tensor.name])
            self.ld(tabs[1][:, 0:TC], zero[0:npart, :], W=[tabs[1].tensor.name])
            for c0 in range(0, T, RC):
                self.ts("dve", rowf[:], rowf0[:], float(c0 // 64), None, op0=ALU.add)
                P_ = slice(0, npart)
                self.ts("dve", ang[P_], rowf[P_], ab[P_, ca:ca + 1], None, op0=ALU.mult)
                self.stt(ang[P_], colf[P_], ab[P_, cb:cb + 1], ang[P_], ALU.mult, ALU.add)
                self.ts("dve", tmp[P_], ang[P_], MAGIC, None, op0=ALU.add)
                self.stt(tmp[P_], tmp[P_], -MAGIC, ang[P_], ALU.add, ALU.subtract)
                self.act(sn[P_], tmp[P_], AF.Sin, scale=-TWO_PI)
                self.stt(tmp[P_], tmp[P_], -1.0, tmp[P_], ALU.mult, ALU.max)
                self.act(cs[P_], tmp[P_], AF.Sin, scale=-TWO_PI, bias=self.cst[P_, 3:4])
                if nonrope:
                    self.memset("dve", cs[0:nonrope, :], 1.0)
                    self.memset("dve", sn[0:nonrope, :], 0.0)
                self.ld(tabs[0][:, TC + c0:TC + c0 + RC], cs[P_, :], W=[tabs[0].tensor.name])
                self.ld(tabs[1][:, TC + c0:TC + c0 + RC], sn[P_, :], W=[tabs[1].tensor.name])
        xT = self.xT[0]
        tiles = [(self.ctx_d, i * 128, i * 128) for i in range(TC // 128)] + [(self.x_d, i * 128, TC + i * 128) for i in range(T // 128)]
        stgs = [sc.sb("xstg0", [128, 8, 512]), sc.sb("xstg1", [128, 8, 512])]
        xins = [sc.sb("xin%d" % i, [128, D]) for i in range(3)]
        for g0 in range(0, len(tiles), 4):
            grp = tiles[g0:g0 + 4]
            stg = stgs[(g0 // 4) % 2]
            for j, (src, r0, t0) in enumerate(grp):
                xin = xins[(g0 + j) % 3]
                self.ld(xin[:], src[r0:r0 + 128, :])
                for half in range(2):
                    pb = self.pb[(2 * (g0 + j) + half) % 8]
                    for q in range(4):
                        k = half * 4 + q
                        self.tr(pb[:, q * 128:(q + 1) * 128], xin[:, k * 128:(k + 1) * 128], self.identF[:])
                    dst = stg[:, half * 4:half * 4 + 4, j * 128:(j + 1) * 128]
                    srcp = pb[:].rearrange("p (q t) -> p q t", q=4)
                    self.cp("act" if half else "dve", dst, srcp, R=[pb], W=[stg])
            n = 128 * len(grp)
            t0 = grp[0][2]
            self.ld(xT[:, t0:t0 + n].rearrange("(k p) t -> p k t", p=128), stg[:, :, 0:n], W=[xT.tensor.name])
        sc.close()

    def phaseM(self, l):
        nc = self.nc
        if not hasattr(self, "mod"):
            g = self.glob
            self.mod = g.sb("mod", [128, 48, 2])
            self.AB = g.sb("AB", [128, 4, 8, 2])
            self.sT = g.sb("sT", [128, 8, 2])
            cv = g.sb("cv", [128, 2, 8])
            self.ld_T(g, cv[:].rearrange("p j k -> p (j k)"), self.cvec_d.rearrange("j (k p) -> (j k) p", p=128), 16, self.pb[1])
            self.act(self.sT[:], cv[:].rearrange("p j k -> p k j"), AF.Silu)
        sc = Scope(self)
        bT = sc.sb("bT", [128, 48])
        self.ld_T(sc, bT[:], self.w["b_mod"][l].rearrange("(f p) -> f p", p=128), 48, self.pb[1])
        nrm = sc.sb("nrm", [128, 2, 8])
        self.ld_T(sc, nrm[:, 0, :], self.w["norm1"][l].rearrange("(k p) -> k p", p=128), 8, self.pb[2])
        self.ld_T(sc, nrm[:, 1, :], self.w["norm2"][l].rearrange("(k p) -> k p", p=128), 8, self.pb[3])
        pacc = self.pb[0]
        wst = [sc.sb("wst0", [128, 8, 768]), sc.sb("wst1", [128, 8, 768])]
        for blk in range(8):
            ws = wst[blk % 2]
            self.ld(ws[:], self.w["w_mod"][l][:, blk * 768:(blk + 1) * 768].rearrange("(k p) n -> p k n", p=128))
            for f in range(6):
                ft = blk * 6 + f
                for k in range(8):
                    self.mm(pacc[:, 2 * ft:2 * ft + 2], ws[:, k, f * 128:(f + 1) * 128], self.sT[:, k, :], start=(k == 0), stop=(k == 7))
        pv = pacc[:, 0:96].rearrange("p (f c) -> p f c", c=2)
        for c in range(2):
            self.tt("dve", self.mod[:, :, c], pv[:, :, c], bT[:], ALU.add, R=[pacc, bT], W=[self.mod])
        for j, (sh_i, sc_i) in enumerate(((0, 1), (3, 4))):
            for c in range(2):
                self.stt(self.AB[:, 2 * j, :, c], self.mod[:, sc_i * 8:(sc_i + 1) * 8, c], 1.0, nrm[:, j, :], ALU.add, ALU.mult,
                         R=[self.mod, nrm], W=[self.AB])
                self.cp("dve", self.AB[:, 2 * j + 1, :, c], self.mod[:, sh_i * 8:(sh_i + 1) * 8, c], R=[self.mod], W=[self.AB])
        sc.close()

    def phaseA(self, l, xT):
        nc, T, NT = self.nc, self.T, self.NT
        W = self.w
        sc = Scope(self)
        win = sc.sb("win", [128, 8, 1568], BF16)
        for k in range(8):
            self.ld(win[:, k, :], W["w_in"][l][k * 128:(k + 1) * 128, :], e="pool", W=[win])
        winr = sc.sb("winr", [128, 8, 544], BF16)
        def rot(dst, src, nh, half):
            d = dst.rearrange("p k (h two x) -> p k h two x", h=nh, two=2, x=half)
            s = src.rearrange("p k (h two x) -> p k h two x", h=nh, two=2, x=half)
            for k in range(8):
                self.ts("dve", d[:, k, :, 0, :], s[:, k, :, 1, :], -1.0, None, op0=ALU.mult, R=[win], W=[winr])
                self.cp("dve", d[:, k, :, 1, :], s[:, k, :, 0, :], R=[win], W=[winr])
        rot(winr[:, :, 0:32], win[:, :, 640:672], 1, 16)
        rot(winr[:, :, 32:416], win[:, :, 928:1312], 6, 32)
        rot(winr[:, :, 416:544], win[:, :, 1312:1440], 2, 32)
        wuq = sc.sb("wuq", [128, 3, 576], BF16)
        self.ld(wuq[:], W["mla_w_uq"][l].rearrange("(k p) n -> p k n", p=128), e="pool")
        wuqr = sc.sb("wuqr", [128, 3, 6, 96], BF16)
        self.memset("dve", wuqr[:], 0.0)
        wq5 = wuq[:].rearrange("p k (h d) -> p k h d", h=6)
        for k in range(3):
            self.ts("dve", wuqr[:, k, :, 64:80], wq5[:, k, :, 80:96], -1.0, None, op0=ALU.mult, R=[wuq], W=[wuqr])
            self.cp("dve", wuqr[:, k, :, 80:96], wq5[:, k, :, 64:80], R=[wuq], W=[wuqr])
        wukv = sc.sb("wukv", [128, 2, 768], BF16)
        self.ld(wukv[:], W["mla_w_ukv"][l].rearrange("(k p) n -> p k n", p=128), e="pool")
        gv = sc.sb("gv", [128, 16])
        self.memset("dve", gv[:], 0.0)
        col = lambda v: v.rearrange("(p o) -> p o", o=1)
        self.ld(gv[:, 0:3], W["mla_q_lora_g"][l].rearrange("(k p) -> p k", p=128), W=[gv])
        self.ld(gv[:, 3:5], W["mla_kv_lora_g"][l].rearrange("(k p) -> p k", p=128), W=[gv])
        qn, kn = W["mla_q_norm"][l], W["mla_k_norm"][l]
        self.ld(gv[0:96, 5:6], col(qn), W=[gv])
        self.ld(gv[64:80, 6:7], col(qn[80:96]), W=[gv])
        self.ld(gv[80:96, 6:7], col(qn[64:80]), W=[gv])
        self.ld(gv[0:64, 7:8], col(kn[0:64]), W=[gv])
        self.ld(gv[64:96, 8:9], col(kn[64:96]), W=[gv])
        self.ld(gv[64:80, 9:10], col(kn[80:96]), W=[gv])
        self.ld(gv[80:96, 9:10], col(kn[64:80]), W=[gv])
        sq_, sk_ = W["swa_q_norm"][l], W["swa_k_norm"][l]
        for hh in range(2):
            b0 = hh * 64
            self.ld(gv[b0:b0 + 64, 10:11], col(sq_), W=[gv])
            self.ld(gv[b0:b0 + 32, 11:12], col(sq_[32:64]), W=[gv])
            self.ld(gv[b0 + 32:b0 + 64, 11:12], col(sq_[0:32]), W=[gv])
            self.ld(gv[b0:b0 + 64, 12:13], col(sk_), W=[gv])
            self.ld(gv[b0:b0 + 32, 13:14], col(sk_[32:64]), W=[gv])
            self.ld(gv[b0 + 32:b0 + 64, 13:14], col(sk_[0:32]), W=[gv])
        cst = self.cst
        NB = 2
        xs = [sc.sb("xs%d" % i, [128, 8, 512]) for i in range(NB)]
        sq = sc.sb("sq", [128, 8, 512], BF16)
        rs = sc.sb("rs", [128, 512])
        tmp = [sc.sb("tmp%d" % i, [128, 512]) for i in range(2)]
        h = sc.sb("h", [128, 8, 512], BF16)
        cqf = sc.sb("cqf", [128, 5, 512])
        sqc = sc.sb("sqc", [128, 5, 512], BF16)
        rsc = sc.sb("rsc", [128, 2, 512])
        cn = sc.sb("cn", [128, 5, 512], BF16)
        tabs = sc.sb("tabs", [128, 4, 512])
        sqh = [sc.sb("sqh%d" % i, [128, 512], BF16) for i in range(2)]
        rq = [sc.sb("rq%d" % i, [128, 512]) for i in range(2)]
        e1 = [sc.sb("e1_%d" % i, [128, 512]) for i in range(2)]
        e2 = [sc.sb("e2_%d" % i, [128, 512]) for i in range(2)]
        qst = sc.sb("qst", [96, 6, 512], BF16)
        kst = sc.sb("kst", [96, 6, 512], BF16)
        krr = sc.sb("krr", [128, 512])
        sqr = sc.sb("sqr", [128, 512], BF16)
        vst = sc.sb("vst", [128, 6, 4, 64], BF16)
        ust = sc.sb("ust", [128, 2, 512])
        usb = sc.sb("usb", [128, 2, 512], BF16)
        qsst = sc.sb("qsst", [128, 3, 512], BF16)
        ksst = sc.sb("ksst", [128, 512], BF16)
        vsst = sc.sb("vsst", [128, 2, 4, 64], BF16)
        pb = self.pb
        pst = [pb[0], pb[1]]
        pm = [pb[2], pb[3], pb[4]]
        pr = [pb[5], pb[6]]
        pv = pb[7]
        cnt = dict(st=0, m=0, r=0, e=0)

        def nxt(key, arr):
            cnt[key] += 1
            return arr[cnt[key] % len(arr)]

        import os
        CUT = int(os.environ.get("KCUT", "99"))
        if CUT <= 1:
            sc.close()
            return
        for ci, (t0, n) in enumerate(self.chunks):
            col_ = 1 if ci == 0 else 0
            x_ = xs[ci % NB]
            self.ld(x_[:, :, 0:n], xT[:, t0:t0 + n].rearrange("(k p) t -> p k t", p=128))
            self.ld(tabs[0:96, 0, 0:n], self.ropeM[0][:, t0:t0 + n], W=[tabs])
            self.ld(tabs[0:96, 1, 0:n], self.ropeM[1][:, t0:t0 + n], W=[tabs])
            self.ld(tabs[:, 2, 0:n], self.ropeS[0][:, t0:t0 + n], W=[tabs])
            self.ld(tabs[:, 3, 0:n], self.ropeS[1][:, t0:t0 + n], W=[tabs])
            for k in range(8):
                self.act(sq[:, k, 0:n], x_[:, k, 0:n], AF.Square)
            p = nxt("st", pst)
            for k in range(8):
                self.mm(p[:, 0:n], self.onesB[:], sq[:, k, 0:n], start=(k == 0), stop=(k == 7))
            self.rstd(rs[:, 0:n], p[:, 0:n], 1.0 / D, cst[:, 0:1])
            for k in range(8):
                t_ = tmp[k % 2]
                self.stt(t_[:, 0:n], x_[:, k, 0:n], self.AB[:, 0, k, col_:col_ + 1], rs[:, 0:n], ALU.mult, ALU.mult)
                self.act(h[:, k, 0:n], t_[:, 0:n], AF.Identity, bias=self.AB[:, 1, k, col_:col_ + 1])

            if CUT <= 2:
                continue
            def proj(dst, wt, c0, m):
                for k in range(8):
                    self.mm(dst[0:m, 0:n], wt[:, k, c0:c0 + m], h[:, k, 0:n], start=(k == 0), stop=(k == 7))

            for m in range(5):
                p = nxt("m", pm)
                proj(p, win, m * 128, 128)
                self.act(sqc[:, m, 0:n], p[:, 0:n], AF.Square)
                self.cp("dve", cqf[:, m, 0:n], p[:, 0:n])
            for (j, ms, dim) in ((0, (0, 1, 2), 384), (1, (3, 4), 256)):
                p = nxt("st", pst)
                for i, m in enumerate(ms):
                    self.mm(p[:, 0:n], self.onesB[:], sqc[:, m, 0:n], start=(i == 0), stop=(i == len(ms) - 1))
                self.rstd(rsc[:, j, 0:n], p[:, 0:n], 1.0 / dim, cst[:, 0:1])
                for m in ms:
                    self.stt(cn[:, m, 0:n], cqf[:, m, 0:n], gv[:, m:m + 1], rsc[:, j, 0:n], ALU.mult, ALU.mult)
            if CUT <= 3:
                continue
            for hd in range(6):
                p = nxt("m", pm)
                r_ = nxt("r", pr)
                for k in range(3):
                    self.mm(p[0:96, 0:n], wuq[:, k, hd * 96:(hd + 1) * 96], cn[:, k, 0:n], start=(k == 0), stop=(k == 2))
                for k in range(3):
                    self.mm(r_[0:96, 0:n], wuqr[:, k, hd, :], cn[:, k, 0:n], start=(k == 0), stop=(k == 2))
                s_ = nxt("e", sqh)
                q_ = rq[cnt["e"] % 2]
                a_ = e1[cnt["e"] % 2]
                b_ = e2[cnt["e"] % 2]
                self.act(s_[0:96, 0:n], p[0:96, 0:n], AF.Square)
                ps_ = nxt("st", pst)
                self.mm(ps_[0:96, 0:n], self.onesB[0:96, 0:96], s_[0:96, 0:n])
                self.rstd(q_[0:96, 0:n], ps_[0:96, 0:n], 1.0, cst[0:96, 1:2])
                self.stt(a_[0:96, 0:n], p[0:96, 0:n], gv[0:96, 5:6], q_[0:96, 0:n], ALU.mult, ALU.mult)
                self.stt(b_[0:96, 0:n], r_[0:96, 0:n], gv[0:96, 6:7], q_[0:96, 0:n], ALU.mult, ALU.mult)
                self.tt("pool", a_[0:96, 0:n], a_[0:96, 0:n], tabs[0:96, 0, 0:n], ALU.mult)
                self.tt("pool", b_[0:96, 0:n], b_[0:96, 0:n], tabs[0:96, 1, 0:n], ALU.mult)
                self.tt("pool", qst[:, hd, 0:n], a_[0:96, 0:n], b_[0:96, 0:n], ALU.add)
            self.ld(self.qT[:, :, t0:t0 + n].rearrange("h p t -> p h t"), qst[:, :, 0:n], W=["qT"])
            if CUT <= 4:
                continue
            p = nxt("m", pm)
            r_ = nxt("r", pr)
            proj(p, win, IN_OFF["kr"], 32)
            proj(r_, winr, 0, 32)
            a_ = e1[0]
            b_ = e2[0]
            self.cp("act", a_[64:96, 0:n], p[0:32, 0:n])
            self.cp("dve", b_[64:96, 0:n], r_[0:32, 0:n])
            self.act(sqr[64:96, 0:n], a_[64:96, 0:n], AF.Square)
            self.ts("dve", a_[64:96, 0:n], a_[64:96, 0:n], gv[64:96, 8:9], None, op0=ALU.mult)
            self.ts("dve", b_[64:96, 0:n], b_[64:96, 0:n], gv[64:96, 9:10], None, op0=ALU.mult)
            self.tt("pool", a_[64:96, 0:n], a_[64:96, 0:n], tabs[64:96, 0, 0:n], ALU.mult)
            self.tt("pool", b_[64:96, 0:n], b_[64:96, 0:n], tabs[64:96, 1, 0:n], ALU.mult)
            self.tt("pool", krr[64:96, 0:n], a_[64:96, 0:n], b_[64:96, 0:n], ALU.add)
            for hd in range(6):
                p = nxt("m", pm)
                for k in range(2):
                    self.mm(p[0:64, 0:n], wukv[:, k, hd * 128:hd * 128 + 64], cn[:, 3 + k, 0:n], start=(k == 0), stop=(k == 1))
                s_ = nxt("e", sqh)
                q_ = rq[cnt["e"] % 2]
                self.act(s_[0:64, 0:n], p[0:64, 0:n], AF.Square)
                ps_ = nxt("st", pst)
                self.mm(ps_[0:96, 0:n], self.onesB[0:64, 0:96], s_[0:64, 0:n], start=True, stop=False)
                self.mm(ps_[0:96, 0:n], self.onesB[64:96, 0:96], sqr[64:96, 0:n], start=False, stop=True)
                self.rstd(q_[0:96, 0:n], ps_[0:96, 0:n], 1.0 / 96, cst[0:96, 0:1])
                self.stt(kst[0:64, hd, 0:n], p[0:64, 0:n], gv[0:64, 7:8], q_[0:64, 0:n], ALU.mult, ALU.mult)
                self.tt("pool", kst[64:96, hd, 0:n], krr[64:96, 0:n], q_[64:96, 0:n], ALU.mult)
            self.ld(self.kT[:, :, t0:t0 + n].rearrange("h p t -> p h t"), kst[:, :, 0:n], W=["kT"])
            if CUT <= 5:
                continue
            wv = wukv[:].rearrange("p k (h d) -> p k h d", h=6)
            for s in range(n // 128):
                pvv = pv[:, 0:384].rearrange("p (h d) -> p h d", h=6)
                for k in range(2):
                    self.mm(pvv, cn[:, 3 + k, s * 128:(s + 1) * 128], wv[:, k, :, 64:128], start=(k == 0), stop=(k == 1), W=[pv])
                self.cp("act", vst[:, :, s, :], pvv, R=[pv], W=[vst])
            self.ld(self.vA[:, :, t0 // 128:(t0 + n) // 128, :].rearrange("h p s d -> p h s d"), vst[:, :, 0:n // 128, :], W=["vA"])
            if CUT <= 6:
                continue
            for m in range(2):
                p = nxt("m", pm)
                proj(p, win, IN_OFF["u"] + m * 128, 128)
                self.cp("act", ust[:, m, 0:n], p[:, 0:n])
                self.cp("dve", usb[:, m, 0:n], p[:, 0:n])
            self.ld(self.uT32[:, t0:t0 + n].rearrange("(m p) t -> p m t", p=128), ust[:, :, 0:n], W=["uT32"])
            self.ld(self.uTb[:, t0:t0 + n].rearrange("(m p) t -> p m t", p=128), usb[:, :, 0:n], W=["uTb"])
            if CUT <= 7:
                continue
            for m in range(4):
                isq = m < 3
                p = nxt("m", pm)
                r_ = nxt("r", pr)
                proj(p, win, (IN_OFF["qs"] + m * 128) if isq else IN_OFF["ks"], 128)
                proj(r_, winr, (32 + m * 128) if isq else 416, 128)
                s_ = nxt("e", sqh)
                q_ = rq[cnt["e"] % 2]
                a_ = e1[cnt["e"] % 2]
                b_ = e2[cnt["e"] % 2]
                self.act(s_[:, 0:n], p[:, 0:n], AF.Square)
                ps_ = nxt("st", pst)
                self.mm(ps_[:, 0:n], self.onesBD[:], s_[:, 0:n])
                if isq:
                    self.rstd(q_[:, 0:n], ps_[:, 0:n], 1.0, cst[:, 2:3])
                else:
                    self.rstd(q_[:, 0:n], ps_[:, 0:n], 1.0 / 64, cst[:, 0:1])
                gc = 10 if isq else 12
                self.stt(a_[:, 0:n], p[:, 0:n], gv[:, gc:gc + 1], q_[:, 0:n], ALU.mult, ALU.mult)
                self.stt(b_[:, 0:n], r_[:, 0:n], gv[:, gc + 1:gc + 2], q_[:, 0:n], ALU.mult, ALU.mult)
                self.tt("pool", a_[:, 0:n], a_[:, 0:n], tabs[:, 2, 0:n], ALU.mult)
                self.tt("pool", b_[:, 0:n], b_[:, 0:n], tabs[:, 3, 0:n], ALU.mult)
                dst = qsst[:, m, 0:n] if isq else ksst[:, 0:n]
                self.tt("pool", dst, a_[:, 0:n], b_[:, 0:n], ALU.add)
            self.ld(self.qsT[:, :, t0:t0 + n].rearrange("(m two) d t -> (two d) m t", two=2), qsst[:, :, 0:n], W=["qsT"])
            self.ld(self.ksT[:, t0:t0 + n], ksst[:, 0:n], W=["ksT"])
            if CUT <= 8:
                continue
            for s in range(n // 128):
                for k in range(8):
                    self.mm(pv[:, 384:512], h[:, k, s * 128:(s + 1) * 128], win[:, k, 1440:1568], start=(k == 0), stop=(k == 7))
                self.cp("dve", vsst[:, :, s, :], pv[:, 384:512].rearrange("p (g d) -> p g d", g=2), R=[pv], W=[vsst])
            self.ld(self.vS[:, :, t0 // 128:(t0 + n) // 128, :].rearrange("g p s d -> p g s d"), vsst[:, :, 0:n // 128, :], W=["vS"])
        sc.close()

    def pipeline(self, items, sk=5):
        n = len(items)
        for i in range(n + sk):
            if i < n:
                if items[i][0]:
                    items[i][0]()
                items[i][1]()
            if i >= sk:
                items[i - sk][2]()
                if items[i - sk][3]:
                    items[i - sk][3]()

    def phaseB(self, l, last):
        NT, NTT = self.NT, self.NTT
        sc = Scope(self)
        kh = [sc.sb("kh%d" % i, [96, NT], BF16) for i in range(2)]
        vh = [sc.sb("vh%d" % i, [128, NTT, 128], BF16) for i in range(2)]
        vstg = [sc.sb("vstg%d" % i, [128, NTT, 64], BF16) for i in range(2)]
        for v in vh:
            self.memset("dve", v[:, :, 64:128], 1.0)
        qc = [sc.sb("qc%d" % i, [96, 512], BF16) for i in range(3)]
        pT = [sc.sb("pT%d" % i, [128, 512], BF16) for i in range(6)]
        yo = [sc.sb("yo%d" % i, [64, 512], BF16) for i in range(2)]
        rd = [sc.sb("rd%d" % i, [64, 512]) for i in range(2)]
        S = self.pb[0:4] + self.pb[6:8]
        acc = self.pb[4:6]
        chunks = self.chunks[1:] if last else self.chunks
        groups = [(hd, ci) for hd in range(6) for ci in range(len(chunks))]

        def ld_head(hd):
            self.ld(kh[hd % 2][:], self.kT[hd])
            self.ld(vstg[hd % 2][:], self.vA[hd])
            self.cp("pool", vh[hd % 2][:, :, 0:64], vstg[hd % 2][:])

        def ld_q(gi):
            hd, ci = groups[gi]
            t0, n = chunks[ci]
            self.ld(qc[gi % 3][:, 0:n], self.qT[hd, :, t0:t0 + n])

        items = []
        it = 0
        for gi, (hd, ci) in enumerate(groups):
            t0, n = chunks[ci]
            kts = list(range(TC // 128)) if t0 < TC else list(range(NTT))
            for j, kt in enumerate(kts):
                pre = None
                if j == 0:
                    def pre(gi=gi, hd=hd, ci=ci):
                        if gi + 1 < len(groups):
                            ld_q(gi + 1)
                if j == 7 and ci == 1:
                    def pre(gi=gi, hd=hd, ci=ci):
                        if hd + 1 < 6:
                            ld_head(hd + 1)

                def s1(it=it, gi=gi, hd=hd, kt=kt, n=n):
                    self.mm(S[it % 6][:, 0:n], kh[hd % 2][:, kt * 128:(kt + 1) * 128], qc[gi % 3][:, 0:n])
                    self.act(pT[it % 6][:, 0:n], S[it % 6][:, 0:n], AF.Exp)

                def s2(it=it, gi=gi, hd=hd, kt=kt, n=n, first=(j == 0), lastk=(j == len(kts) - 1)):
                    self.mm(acc[gi % 2][:, 0:n], vh[hd % 2][:, kt, :], pT[it % 6][:, 0:n], start=first, stop=lastk)

                fin = None
                if j == len(kts) - 1:
                    def fin(gi=gi, hd=hd, t0=t0, n=n):
                        a, r_, y_ = acc[gi % 2], rd[gi % 2], yo[gi % 2]
                        self.cp("act", r_[:, 0:n], a[64:128, 0:n])
                        self.recip(r_[:, 0:n], r_[:, 0:n])
                        self.tt("dve", y_[:, 0:n], a[0:64, 0:n], r_[:, 0:n], ALU.mult)
                        self.ld(self.ycT[hd * 64:(hd + 1) * 64, t0:t0 + n], y_[:, 0:n], W=["ycT"])
                items.append((pre, s1, s2, fin))
                it += 1
        ld_head(0)
        ld_q(0)
        self.pipeline(items)
        sc.close()

    def phaseC(self, l, last):
        NT, NTT, T = self.NT, self.NTT, self.T
        sc = Scope(self)
        ks = sc.sb("ks", [128, NT], BF16)
        self.ld(ks[:], self.ksT)
        vs = sc.sb("vs", [128, 2, NTT, 128], BF16)
        vsg = sc.sb("vsg", [128, 2, NTT, 64], BF16)
        self.memset("dve", vs[:, :, :, 64:128], 1.0)
        for g in range(2):
            self.ld(vsg[:, g], self.vS[g], W=[vsg])
            self.cp("pool", vs[:, g, :, 0:64], vsg[:, g])
        m3 = [sc.sb("mlo3", [128, 3, 128], BF16), sc.sb("mhi3", [128, 3, 128], BF16)]
        for j in range(3):
            self.cp("dve", m3[0][:, j, :], self.mlo[:])
            self.cp("dve", m3[1][:, j, :], self.mhi[:])
        sk = sc.sb("sk", [128, 6])
        self.ld(sk[:], self.w["swa_sink"][l].partition_broadcast(128))
        self.act(sk[:], sk[:], AF.Exp)
        es = sc.sb("es", [128, 2, 3, 128])
        self.memset("dve", es[:], 0.0)
        for g in range(2):
            for j in range(3):
                self.ts("dve", es[:, g, j, :], es[:, g, j, :], sk[:, 3 * g + j:3 * g + j + 1], None, op0=ALU.add)
        qb = [sc.sb("qb%d" % i, [128, 3, 512], BF16) for i in range(2)]
        pT = [sc.sb("pTs%d" % i, [128, 384], BF16) for i in range(6)]
        yo = [sc.sb("yos%d" % i, [64, 2, 3, 512], BF16) for i in range(2)]
        rd = [sc.sb("rds%d" % i, [128, 384]) for i in range(2)]
        S = self.pb[0:4] + self.pb[6:8]
        acc = self.pb[4:6]
        chunks = self.chunks[1:] if last else self.chunks

        def ld_q(ci):
            t0, n = chunks[ci]
            for g in range(2):
                self.ld(qb[ci % 2][g * 64:(g + 1) * 64, :, 0:n], self.qsT[3 * g:3 * g + 3, :, t0:t0 + n].rearrange("j d t -> d j t"), W=[qb[ci % 2]])

        items = []
        it = 0
        gi = 0
        for ci, (t0, n) in enumerate(chunks):
            for qt in range(n // 128):
                tile = t0 // 128 + qt
                if t0 < TC:
                    kts = [(0, None), (1, None)]
                else:
                    b = tile - TC // 128
                    kts = [(0, None), (1, None)]
                    if b >= 1:
                        kts.append((tile - 1, 0))
                    kts.append((tile, None))
                    if b + 1 < T // 128:
                        kts.append((tile + 1, 1))
                for g in range(2):
                    for j, (kt, msk) in enumerate(kts):
                        pre = None
                        if j == 0 and qt == 0 and g == 0:
                            def pre(ci=ci):
                                if ci + 1 < len(chunks):
                                    ld_q(ci + 1)

                        def s1(it=it, ci=ci, qt=qt, g=g, kt=kt, msk=msk):
                            P_ = slice(g * 64, (g + 1) * 64)
                            sv = S[it % 6][:, 0:384].rearrange("p (j t) -> p j t", j=3)
                            self.mm(sv, ks[P_, kt * 128:(kt + 1) * 128], qb[ci % 2][P_, :, qt * 128:(qt + 1) * 128], W=[S[it % 6]])
                            self.act(pT[it % 6][:], S[it % 6][:, 0:384], AF.Exp)
                            if msk is not None:
                                self.tt("dve", pT[it % 6][:], pT[it % 6][:], m3[msk][:].rearrange("p j t -> p (j t)"), ALU.mult)

                        def s2(it=it, gi=gi, g=g, kt=kt, first=(j == 0), lastk=(j == len(kts) - 1)):
                            self.mm(acc[gi % 2][:, 0:384], vs[:, g, kt, :], pT[it % 6][:], start=first, stop=lastk)

                        fin = None
                        if j == len(kts) - 1:
                            def fin(gi=gi, g=g, ci=ci, qt=qt, t0=t0, n=n):
                                a, r_, y_ = acc[gi % 2], rd[gi % 2], yo[ci % 2]
                                self.tt("dve", r_[64:128, :], a[64:128, 0:384], es[64:128, g].rearrange("p j t -> p (j t)"), ALU.add)
                                self.cp("act", r_[0:64, :], r_[64:128, :])
                                self.recip(r_[0:64, :], r_[0:64, :])
                                self.tt("dve", y_[:, g, :, qt * 128:(qt + 1) * 128], a[0:64, 0:384].rearrange("p (j t) -> p j t", j=3),
                                        r_[0:64, :].rearrange("p (j t) -> p j t", j=3), ALU.mult, R=[a, r_], W=[y_])
                                if qt == n // 128 - 1 and g == 1:
                                    for gg in range(2):
                                        r0 = 640 + gg * 192
                                        self.ld(self.ycT[r0:r0 + 192, t0:t0 + n].rearrange("(j d) t -> d j t", j=3), y_[:, gg, :, 0:n], W=["ycT"])
                        items.append((pre, s1, s2, fin))
                        it += 1
                    gi += 1
        ld_q(0)
        self.pipeline(items)
        sc.close()

    def ldT2(self, sc, dst, src_rows, n, m, pbank):
        if getattr(self, "_ldT2_sc", None) is not sc:
            self._ldT2_sc = sc
            self._ldT2_stg = [sc.sb("ldT2_%d" % i, [128, 128]) for i in range(2)]
            self._ldT2_n = 0
        self._ldT2_n += 1
        stg = self._ldT2_stg[self._ldT2_n % 2]
        self.ld(stg[0:n, 0:m], src_rows)
        self.tr(pbank[0:m, 0:n], stg[0:n, 0:m], self.identF[0:n, 0:n])
        self.cp("dve", dst, pbank[0:m, 0:n])

    def phaseD(self, l, last):
        nc, T, NT = self.nc, self.T, self.NT
        W = self.w
        sc = Scope(self)
        TPI = 6.28318
        pb = self.pb
        sm = lambda name, shape=(128, 16): sc.sb(name, list(shape))
        aT = sm("aT", (64, 2, 32))
        self.ldT2(sc, aT[:, 0, :], W["s5_a_re"][l].rearrange("d g p -> (d g) p"), 32, 64, pb[0])
        self.ldT2(sc, aT[:, 1, :], W["s5_a_im"][l].rearrange("d g p -> (d g) p"), 32, 64, pb[1])
        are, aim, dtl = sm("are"), sm("aim"), sm("dtl")
        a4 = aT[:].rearrange("p r (dk gl) -> p r dk gl", gl=2)
        for (dst, ri) in ((are, 0), (aim, 1)):
            self.cp("dve", dst[0:64, :], a4[:, ri, :, 0])
            self.cp("act", dst[64:128, :], a4[:, ri, :, 1])
        ldt = sm("ldt", (128, 32))
        self.ld(ldt[:], W["s5_log_dt"][l].rearrange("d g -> (d g)").partition_broadcast(128))
        l3 = ldt[:].rearrange("p (dk gl) -> p dk gl", gl=2)
        self.act(dtl[0:64, :], l3[0:64, :, 0], AF.Exp)
        self.act(dtl[64:128, :], l3[64:128, :, 1], AF.Exp)
        rr, phi, w_, nf, sn0, cs0 = sm("rr"), sm("phi"), sm("w_"), sm("nf"), sm("sn0"), sm("cs0")
        self.tt("dve", rr[:], are[:], dtl[:], ALU.mult)
        self.act(rr[:], rr[:], AF.Exp)
        self.tt("dve", phi[:], aim[:], dtl[:], ALU.mult)
        self.ts("dve", phi[:], phi[:], 1.0 / TWO_PI, None, op0=ALU.mult)
        self.ts("dve", w_[:], phi[:], MAGIC, None, op0=ALU.add)
        self.stt(nf[:], w_[:], -MAGIC, phi[:], ALU.add, ALU.subtract)
        self.act(sn0[:], nf[:], AF.Sin, scale=-TPI)
        self.stt(nf[:], nf[:], -1.0, nf[:], ALU.mult, ALU.max)
        self.act(cs0[:], nf[:], AF.Sin, scale=-TPI, bias=self.cst[:, 3:4])
        nr, ni, den, cre, cim, ncim, t_ = sm("nr"), sm("ni"), sm("den"), sm("cre"), sm("cim"), sm("ncim"), sm("t_")
        self.tt("dve", nr[:], rr[:], cs0[:], ALU.mult)
        self.ts("dve", nr[:], nr[:], -1.0, None, op0=ALU.add)
        self.tt("dve", ni[:], rr[:], sn0[:], ALU.mult)
        self.tt("dve", den[:], are[:], are[:], ALU.mult)
        self.tt("dve", t_[:], aim[:], aim[:], ALU.mult)
        self.tt("dve", den[:], den[:], t_[:], ALU.add)
        self.recip(den[:], den[:])
        self.tt("dve", cre[:], nr[:], are[:], ALU.mult)
        self.tt("dve", t_[:], ni[:], aim[:], ALU.mult)
        self.tt("dve", cre[:], cre[:], t_[:], ALU.add)
        self.tt("dve", cre[:], cre[:], den[:], ALU.mult)
        self.tt("dve", cim[:], ni[:], are[:], ALU.mult)
        self.tt("dve", t_[:], nr[:], aim[:], ALU.mult)
        self.tt("dve", cim[:], cim[:], t_[:], ALU.subtract)
        self.tt("dve", cim[:], cim[:], den[:], ALU.mult)
        self.ts("dve", ncim[:], cim[:], -1.0, None, op0=ALU.mult)
        phh, phl = sm("phh"), sm("phl")
        self.ts("dve", t_[:], phi[:], 1024.0, MAGIC, op0=ALU.mult, op1=ALU.add)
        self.ts("dve", phh[:], t_[:], -MAGIC, 1.0 / 1024.0, op0=ALU.add, op1=ALU.mult)
        self.tt("dve", phl[:], phi[:], phh[:], ALU.subtract)
        if "dbgP" in self.dbg:
            dP = self.nc.dram_tensor("dbgP", [128, 16, 16], F32, kind="ExternalOutput").ap()
            for i, t in enumerate((are, aim, dtl, rr, phi, sn0, cs0, cre, cim, phh, phl)):
                self.ld(dP[:, i, :], t[:], W=["dbgP"])
        Bri = sc.sb("Bri", [128, 2, 16, 16])
        for ri, nm in enumerate(("s5_b_re", "s5_b_im")):
            for d in range(2):
                for gl in range(2):
                    self.ld(Bri[gl * 64:(gl + 1) * 64, ri, d * 8:(d + 1) * 8, :],
                            W[nm][l, d].rearrange("(k gl) p h -> gl p k h", gl=2)[gl], W=[Bri])
        bb = sc.sb("bb", [128, 2, 16, 16])
        for dk in range(16):
            c1, c2, c3 = cre[:, dk:dk + 1], cim[:, dk:dk + 1], ncim[:, dk:dk + 1]
            self.ts("dve", bb[:, 0, dk, :], Bri[:, 0, dk, :], c1, None, op0=ALU.mult)
            self.stt(bb[:, 0, dk, :], Bri[:, 1, dk, :], c3, bb[:, 0, dk, :], ALU.mult, ALU.add)
            self.ts("dve", bb[:, 1, dk, :], Bri[:, 0, dk, :], c2, None, op0=ALU.mult)
            self.stt(bb[:, 1, dk, :], Bri[:, 1, dk, :], c1, bb[:, 1, dk, :], ALU.mult, ALU.add)
        BbT = sc.sb("BbT", [128, 32, 128], BF16)
        Et = [sc.sb("Et%d" % i, [128, 64]) for i in range(2)]
        for e_ in Et:
            self.memset("dve", e_[:], 0.0)
        for dk in range(16):
            k = dk % 8
            hi = (k % 4 == 3)
            p0 = 64 if hi else (32 * k) % 128
            nr_ = 64 if hi else 32
            c0 = 32 if hi else 0
            for ri in range(2):
                e_ = Et[ri]
                self.cp("dve", e_[0:64, c0:c0 + 16], bb[0:64, ri, dk, :])
                self.cp("dve", e_[64:128, c0 + 16:c0 + 32], bb[64:128, ri, dk, :])
                pbk = pb[2 + ri]
                self.tr(pbk[0:nr_, 0:128], e_[:, 0:nr_], self.identF[:])
                self.cp("act", BbT[p0:p0 + nr_, dk * 2 + ri, :], pbk[0:nr_, 0:128])
                if c0 == 0 and k % 4 == 2:
                    pass
                if hi or True:
                    self.memset("dve", e_[:, c0:c0 + 32], 0.0)
        CT = sc.sb("CT", [64, 2, 512])
        for ri, nm in enumerate(("s5_c_re", "s5_c_im")):
            rows = W[nm][l].rearrange("d g h p -> (d g h) p")
            for blk in range(4):
                self.ldT2(sc, CT[:, ri, blk * 128:(blk + 1) * 128], rows[blk * 128:(blk + 1) * 128, :], 128, 64, pb[4 + blk % 2])
        Cm = sc.sb("Cm", [128, 48, 128], BF16)
        self.memset("dve", Cm[:], 0.0)
        for dk in range(16):
            d, k = dk // 8, dk % 8
            for pl, (ri, sg_) in enumerate(((0, 1.0), (1, -1.0), (0, -1.0))):
                for gl in range(2):
                    g = 2 * k + gl
                    c0 = (k % 4) * 32 + gl * 16
                    src = CT[0:64, ri, (d * 16 + g) * 16:(d * 16 + g) * 16 + 16]
                    self.act(Cm[gl * 64:(gl + 1) * 64, dk * 3 + pl, c0:c0 + 16], src, AF.Copy, scale=sg_)
        dsk, bgl = sm("dsk", (128, 2)), sm("bgl", (128, 2))
        self.ldT2(sc, dsk[:], W["s5_d"][l].rearrange("(c p) -> c p", p=128), 2, 128, pb[6])
        self.ldT2(sc, bgl[:], W["s5_b_glu"][l].rearrange("(c p) -> c p", p=128), 2, 128, pb[7])
        wgl = sc.sb("wgl", [128, 2, 256], BF16)
        self.ld(wgl[:], W["s5_w_glu"][l].rearrange("(k p) n -> p k n", p=128), e="pool")
        ji = sc.sb("ji", [128, 512], I32)
        jrow = sc.sb("jrow", [128, 512])
        self.fw.op("pool", lambda: nc.gpsimd.iota(ji[:], pattern=[[1, 512]], base=0, channel_multiplier=0), [], [ji])
        self.cp("dve", jrow[:], ji[:])
        J = sc.sb("J", [128, 8, 512])
        rmul = sc.sb("rmul", [128, 8, 512])
        tq = [sc.sb("tq%d" % i, [128, 512]) for i in range(2)]

        def build_tables(d):
            for k in range(8):
                dk = d * 8 + k
                v, w2 = tq
                self.ts("dve", v[:], jrow[:], phh[:, dk:dk + 1], None, op0=ALU.mult)
                self.ts("dve", w2[:], v[:], MAGIC, None, op0=ALU.add)
                self.stt(w2[:], w2[:], -MAGIC, v[:], ALU.add, ALU.subtract)
                self.stt(J[:, k, :], jrow[:], phl[:, dk:dk + 1], w2[:], ALU.mult, ALU.subtract)
                self.ts("pool", rmul[:, k, :], jrow[:], 0.0, rr[:, dk:dk + 1], op0=ALU.mult, op1=ALU.add)
        ub = sc.sb("ub", [128, 2, NT], BF16)
        self.ld(ub[:], self.uTb.rearrange("(c p) t -> p c t", p=128))
        R2 = lambda name, dt=F32: [sc.sb("%s%d" % (name, i), [128, 512], dt) for i in range(2)]
        R1 = lambda name: [sc.sb(name, [128, 512])] * 2
        cs_, sn_, w1_, nf_ = R2("cs"), R2("sn"), R1("w1"), R2("nfm")
        bre_, bim_, zr_, zi_, t1_, t2_ = R2("bre"), R2("bim"), R1("zr"), R1("zi"), R1("t1"), R1("t2")
        t3_, t4_ = R1("t3"), R1("t4")
        zre_, zim_, u1_ = R1("zre"), R1("zim"), R2("u1")
        hh = [sc.sb("hh%d" % k, [128, 4, 512], BF16) for k in range(8)]
        carry = sc.sb("carry", [128, 2, 8, 2])
        bs = [sc.sb("bs%d" % i, [128, 5, 8]) for i in range(2)]
        u32 = [sc.sb("u32_%d" % i, [128, 2, 512]) for i in range(2)]
        ys = [sc.sb("ys%d" % i, [128, 2, 512]) for i in range(2)]
        zt = [sc.sb("zt", [128, 2, 512])] * 2
        zb = [sc.sb("zb", [128, 2, 512], BF16)] * 2
        yo = [sc.sb("yod", [128, 2, 512], BF16)] * 2
        nlat = T // 512
        units = []
        chunk_ctx = {}
        uidx = {}

        def finish_chunk(d, oi):
            t0, n, b_ = chunk_ctx[(d, oi)]
            for ct in range(2):
                acc = pb[4 + ct]
                lst = [(k, pl) for k in range(4 * ct, 4 * ct + 4) for pl in range(4)]
                for i, (k, pl) in enumerate(lst):
                    cpl = (0, 2, 1, 1)[pl]
                    self.mm(acc[:, 0:n], Cm[:, (d * 8 + k) * 3 + cpl, :], hh[k][:, pl, 0:n], start=(i == 0), stop=(i == len(lst) - 1))
                if d == 0:
                    self.stt(ys[oi % 2][:, ct, 0:n], u32[oi % 2][:, ct, 0:n], dsk[:, ct:ct + 1], acc[:, 0:n], ALU.mult, ALU.add)
                else:
                    self.tt("dve", ys[oi % 2][:, ct, 0:n], ys[oi % 2][:, ct, 0:n], acc[:, 0:n], ALU.add)
            if d == 0:
                self.ld(self.ysum[:, t0:t0 + n].rearrange("(c p) t -> p c t", p=128), ys[oi % 2][:, :, 0:n], W=["ysum"])
            else:
                y_, z_, zb_ = ys[oi % 2], zt[oi % 2], zb[oi % 2]
                for ct in range(2):
                    y1, z1 = y_[:, ct, 0:n], z_[:, ct, 0:n]
                    self.tt("pool", z1, y1, y1, ALU.mult)
                    self.ts("dve", z1, z1, 0.044715, 1.0, op0=ALU.mult, op1=ALU.add)
                    self.tt("pool", z1, z1, y1, ALU.mult)
                    self.act(z1, z1, AF.Tanh, scale=math.sqrt(2.0 / math.pi))
                    self.stt(z1, z1, 1.0, y1, ALU.add, ALU.mult)
                    self.ts("dve", z1, z1, 0.5, None, op0=ALU.mult)
                    self.cp("act", zb_[:, ct, 0:n], z1)
                for ct in range(2):
                    pg = pb[6 + ct]
                    for kk in range(2):
                        self.mm(pg[:, 0:n], wgl[:, kk, ct * 128:(ct + 1) * 128], zb_[:, kk, 0:n], start=(kk == 0), stop=(kk == 1))
                    gt = u1_[ct]
                    self.act(gt[:, 0:n], pg[:, 0:n], AF.Sigmoid, bias=bgl[:, ct:ct + 1])
                    self.tt("dve", yo[oi % 2][:, ct, 0:n], z_[:, ct, 0:n], gt[:, 0:n], ALU.mult)
                self.ld(self.ycT[384:640, t0:t0 + n].rearrange("(c p) t -> p c t", p=128), yo[oi % 2][:, :, 0:n], W=["ycT"])

        def emit_T(u):
            d, oi, k = u
            t0, n, b_ = chunk_ctx[(d, oi)]
            sgn = 1.0 if d == 0 else -1.0
            dk = d * 8 + k
            i2 = uidx.setdefault(u, len(uidx)) % 2
            cs, sn, w1, nfm = cs_[i2], sn_[i2], w1_[i2], nf_[i2]
            base, b2p, nbase = b_[:, 2, k:k + 1], b_[:, 3, k:k + 1], b_[:, 4, k:k + 1]
            hi = (k % 4 == 3)
            p0 = 64 if hi else (32 * k) % 128
            p1 = p0 + (64 if hi else 32)
            c_ = (32 * k) // 128
            un = uidx[u]
            pre_, pim_ = pb[(2 * un) % 4], pb[(2 * un + 1) % 4]
            return [
                lambda: self.ts("dve", w1[:, 0:n], J[:, k, 0:n], base, MAGIC, op0=ALU.add, op1=ALU.add),
                lambda: self.mm(pre_[:, 0:n], BbT[p0:p1, dk * 2, :], ub[p0:p1, c_, t0:t0 + n]),
                lambda: self.stt(nfm[:, 0:n], w1[:, 0:n], -MAGIC, J[:, k, 0:n], ALU.add, ALU.subtract),
                lambda: self.mm(pim_[:, 0:n], BbT[p0:p1, dk * 2 + 1, :], ub[p0:p1, c_, t0:t0 + n]),
                lambda: self.act(sn[:, 0:n], nfm[:, 0:n], AF.Sin, scale=-sgn * TPI, bias=b2p),
                lambda: self.cp("act", bre_[i2][:, 0:n], pre_[:, 0:n]),
                lambda: self.act(w1[:, 0:n], nfm[:, 0:n], AF.Abs, bias=nbase),
                lambda: self.cp("act", bim_[i2][:, 0:n], pim_[:, 0:n]),
                lambda: self.act(cs[:, 0:n], w1[:, 0:n], AF.Sin, scale=-TPI, bias=self.cst[:, 3:4]),
            ]

        def emit_R(u):
            d, oi, k = u
            t0, n, b_ = chunk_ctx[(d, oi)]
            i2 = uidx[u] % 2
            cs, sn, bre, bim = cs_[i2], sn_[i2], bre_[i2], bim_[i2]
            zr, zi, t1, t2, t3, t4 = zr_[0], zi_[0], t1_[0], t2_[0], t3_[0], t4_[0]
            zre, zim = zre_[0], zim_[0]
            rv = (lambda a: a[:, 0:n]) if d == 0 else (lambda a: a[:, 0:n][:, ::-1])
            H = hh[k]

            def sc_(zin, zout, ri):
                init = 0.0 if oi == 0 else carry[:, d, k, ri:ri + 1]
                self.scan(rv(zout), rmul[:, k, 0:n], rv(zin), init, R=[rmul, zin, carry], W=[zout])

            def cy_(zout, ri):
                lastc = zout[:, n - 1:n] if d == 0 else zout[:, 0:1]
                self.cp("dve", carry[:, d, k, ri:ri + 1], lastc)

            ops = [
                lambda: self.tt("dve", t1[:, 0:n], bre[:, 0:n], cs[:, 0:n], ALU.mult),
                lambda: self.tt("pool", t3[:, 0:n], bim[:, 0:n], cs[:, 0:n], ALU.mult),
                lambda: self.tt("dve", t2[:, 0:n], bim[:, 0:n], sn[:, 0:n], ALU.mult),
                lambda: self.tt("pool", t4[:, 0:n], bre[:, 0:n], sn[:, 0:n], ALU.mult),
                lambda: self.tt("dve", zr[:, 0:n], t1[:, 0:n], t2[:, 0:n], ALU.add),
                lambda: self.tt("pool", zi[:, 0:n], t3[:, 0:n], t4[:, 0:n], ALU.subtract),
                lambda: sc_(zr, zre, 0),
                lambda: cy_(zre, 0),
                lambda: sc_(zi, zim, 1),
                lambda: cy_(zim, 1),
                lambda: self.tt("dve", H[:, 0, 0:n], zre[:, 0:n], cs[:, 0:n], ALU.mult),
                lambda: self.tt("pool", H[:, 2, 0:n], zre[:, 0:n], sn[:, 0:n], ALU.mult),
                lambda: self.tt("dve", H[:, 1, 0:n], zim[:, 0:n], sn[:, 0:n], ALU.mult),
                lambda: self.tt("pool", H[:, 3, 0:n], zim[:, 0:n], cs[:, 0:n], ALU.mult),
            ]
            if k == 7:
                ops.append(lambda: finish_chunk(d, oi))
            return ops

        def run_merged(a, b):
            i = j = 0
            while i < len(a) or j < len(b):
                if i < len(a):
                    a[i]()
                    i += 1
                if j < len(b):
                    b[j]()
                    j += 1

        for d in range(2):
            sgn = 1.0 if d == 0 else -1.0
            build_tables(d)
            order = list(range(len(self.chunks))) if d == 0 else [0] + list(range(len(self.chunks) - 1, 0, -1))
            for oi, ci in enumerate(order):
                t0, n = self.chunks[ci]
                rho0 = float(t0) if d == 0 else (float(T) if ci == 0 else float(t0 - TC))
                b_ = bs[oi % 2]
                dsl = slice(d * 8, d * 8 + 8)
                self.ts("dve", b_[:, 0, :], phh[:, dsl], rho0, None, op0=ALU.mult)
                self.ts("dve", b_[:, 1, :], b_[:, 0, :], MAGIC, None, op0=ALU.add)
                self.stt(b_[:, 1, :], b_[:, 1, :], -MAGIC, b_[:, 0, :], ALU.add, ALU.subtract)
                self.stt(b_[:, 2, :], phl[:, dsl], rho0, b_[:, 1, :], ALU.mult, ALU.subtract)
                self.ts("dve", b_[:, 3, :], b_[:, 2, :], sgn * TPI, None, op0=ALU.mult)
                self.ts("dve", b_[:, 4, :], b_[:, 2, :], -1.0, None, op0=ALU.mult)
                if d == 0:
                    self.ld(u32[oi % 2][:, :, 0:n], self.uT32[:, t0:t0 + n].rearrange("(c p) t -> p c t", p=128))
                else:
                    self.ld(ys[oi % 2][:, :, 0:n], self.ysum[:, t0:t0 + n].rearrange("(c p) t -> p c t", p=128))
                chunk_ctx[(d, oi)] = (t0, n, b_)
                for k in range(8):
                    units.append((d, oi, k))
                    tl = emit_T(units[-1])
                    rl = emit_R(units[-2]) if len(units) >= 2 else []
                    run_merged(rl, tl)
                    if k == 7 and oi == len(order) - 1:
                        run_merged(emit_R(units[-1]), [])
                        units.clear()
                if "dbgD" in self.dbg:
                    pass
        sc.close()

    def phaseE(self, l, xin, xout, last):
        nc, T, NT = self.nc, self.T, self.NT
        W = self.w
        pb = self.pb
        sc = Scope(self)
        wout = sc.sb("wout", [128, 8, 1024], BF16)
        for k in range(8):
            self.ld(wout[:, k, :], W["w_out"][l][k * 128:(k + 1) * 128, :], e="pool", W=[wout])
        yc = [sc.sb("yc%d" % i, [128, 8, 512], BF16) for i in range(2)]
        xs = [sc.sb("xe%d" % i, [128, 8, 512]) for i in range(2)]
        xo = [sc.sb("xo%d" % i, [128, 8, 512]) for i in range(2)]
        chunks = self.chunks[1:] if last else self.chunks

        def ld1(ci):
            t0, n = chunks[ci]
            self.ld(yc[ci % 2][:, :, 0:n], self.ycT[:, t0:t0 + n].rearrange("(k p) t -> p k t", p=128))
            self.ld(xs[ci % 2][:, :, 0:n], xin[:, t0:t0 + n].rearrange("(k p) t -> p k t", p=128))

        ld1(0)
        nb = 0
        for ci, (t0, n) in enumerate(chunks):
            col_ = 1 if t0 < TC else 0
            if ci + 1 < len(chunks):
                ld1(ci + 1)
            for m in range(8):
                p = pb[nb % 8]
                nb += 1
                for k in range(8):
                    self.mm(p[:, 0:n], wout[:, k, m * 128:(m + 1) * 128], yc[ci % 2][:, k, 0:n], start=(k == 0), stop=(k == 7))
                self.stt(xo[ci % 2][:, m, 0:n], p[:, 0:n], self.mod[:, 16 + m, col_:col_ + 1], xs[ci % 2][:, m, 0:n], ALU.mult, ALU.add)
            self.ld(xout[:, t0:t0 + n].rearrange("(k p) t -> p k t", p=128), xo[ci % 2][:, :, 0:n], W=[xout.tensor.name])
        sc.close()
        sc = Scope(self)
        wup = sc.sb("wup", [128, 8, 2 * DFF], BF16)
        for k in range(8):
            for q in range(4):
                self.ld(wup[:, k, q * 1408:(q + 1) * 1408], W["w_up"][l][k * 128:(k + 1) * 128, q * 1408:(q + 1) * 1408], e="pool", W=[wup])
        wdn = sc.sb("wdn", [128, 22, 1024], BF16)
        for f in range(22):
            self.ld(wdn[:, f, :], W["w_down"][l][f * 128:(f + 1) * 128, :], e="pool", W=[wdn])
        cw = sc.sb("cw", [128, 132])
        cwr = W["conv_w"][l].rearrange("j (f p) -> (j f) p", p=128)
        self.ldT2(sc, cw[:, 0:128], cwr[0:128, :], 128, 128, pb[0])
        self.ldT2(sc, cw[:, 128:132], cwr[128:132, :], 4, 128, pb[1])
        cb = sc.sb("cb", [128, 44])
        self.ldT2(sc, cb[:], W["conv_b"][l].rearrange("(f p) -> f p", p=128), 44, 128, pb[2])
        WN = 256
        xw = [sc.sb("xw%d" % i, [128, 8, WN]) for i in range(2)]
        sq = sc.sb("sqe", [128, 8, WN], BF16)
        rs = sc.sb("rse", [128, WN])
        tmp = [sc.sb("tme%d" % i, [128, WN]) for i in range(2)]
        h2 = sc.sb("h2", [128, 8, WN], BF16)
        actT = sc.sb("actT", [128, 22, WN], BF16)
        ta = [sc.sb("ta%d" % i, [128, WN]) for i in range(2)]
        tg = [sc.sb("tg%d" % i, [128, WN]) for i in range(2)]
        sg = [sc.sb("sg%d" % i, [128, WN]) for i in range(2)]
        xo2 = sc.sb("xo2", [128, 8, WN])
        wins = []
        seqs = ([] if last else [(0, TC, 1)]) + [(TC, T, 0)]
        for (off, Ls, col_) in seqs:
            for s0 in range(0, Ls, 254):
                wins.append((off, Ls, col_, s0, min(254, Ls - s0)))

        def ldw(wi):
            off, Ls, col_, s0, nv = wins[wi]
            lo, hi = (s0 == 0), (s0 + nv == Ls)
            c0, c1 = (1 if lo else 0), (nv + 1 if hi else nv + 2)
            self.ld(xw[wi % 2][:, :, c0:c1], xout[:, off + s0 - 1 + c0:off + s0 - 1 + c1].rearrange("(k p) t -> p k t", p=128))

        ldw(0)
        nb = 0
        for wi, (off, Ls, col_, s0, nv) in enumerate(wins):
            lo, hi = (s0 == 0), (s0 + nv == Ls)
            c0, c1 = (1 if lo else 0), (nv + 1 if hi else nv + 2)
            nw = nv + 2
            x_ = xw[wi % 2]
            if wi + 1 < len(wins):
                ldw(wi + 1)
            for k in range(8):
                self.act(sq[:, k, c0:c1], x_[:, k, c0:c1], AF.Square)
            pst = pb[6 + wi % 2]
            for k in range(8):
                self.mm(pst[:, c0:c1], self.onesB[:], sq[:, k, c0:c1], start=(k == 0), stop=(k == 7))
            self.rstd(rs[:, c0:c1], pst[:, c0:c1], 1.0 / D, self.cst[:, 0:1])
            if lo:
                self.memset("pool", h2[:, :, 0:1], 0.0)
            if hi:
                self.memset("pool", h2[:, :, nv + 1:nv + 2], 0.0)
            for k in range(8):
                t_ = tmp[k % 2]
                self.stt(t_[:, c0:c1], x_[:, k, c0:c1], self.AB[:, 2, k, col_:col_ + 1], rs[:, c0:c1], ALU.mult, ALU.mult)
                self.act(h2[:, k, c0:c1], t_[:, c0:c1], AF.Identity, bias=self.AB[:, 3, k, col_:col_ + 1])
            for f in range(22):
                pa, pg = pb[(2 * nb) % 6], pb[(2 * nb + 1) % 6]
                a_, g_, s_ = ta[nb % 2], tg[nb % 2], sg[nb % 2]
                nb += 1
                for k in range(8):
                    self.mm(pa[:, 0:nw], wup[:, k, f * 128:(f + 1) * 128], h2[:, k, 0:nw], start=(k == 0), stop=(k == 7))
                for k in range(8):
                    self.mm(pg[:, 0:nw], wup[:, k, DFF + f * 128:DFF + (f + 1) * 128], h2[:, k, 0:nw], start=(k == 0), stop=(k == 7))
                for (pp, tt_, ft) in ((pa, a_, f), (pg, g_, 22 + f)):
                    w0, w1, w2 = cw[:, ft:ft + 1], cw[:, 44 + ft:45 + ft], cw[:, 88 + ft:89 + ft]
                    self.act(tt_[:, 0:nv], pp[:, 1:nv + 1], AF.Identity, scale=w1, bias=cb[:, ft:ft + 1])
                    self.stt(tt_[:, 0:nv], pp[:, 0:nv], w0, tt_[:, 0:nv], ALU.mult, ALU.add)
                    self.stt(tt_[:, 0:nv], pp[:, 2:nv + 2], w2, tt_[:, 0:nv], ALU.mult, ALU.add)
                self.act(s_[:, 0:nv], g_[:, 0:nv], AF.Silu)
                self.tt("pool", actT[:, f, 0:nv], s_[:, 0:nv], a_[:, 0:nv], ALU.mult)
            for m in range(8):
                pd = pb[6 + m % 2]
                for f in range(22):
                    self.mm(pd[:, 0:nv], wdn[:, f, m * 128:(m + 1) * 128], actT[:, f, 0:nv], start=(f == 0), stop=(f == 21))
                self.stt(xo2[:, m, 0:nv], pd[:, 0:nv], self.mod[:, 40 + m, col_:col_ + 1], x_[:, m, 1:nv + 1], ALU.mult, ALU.add)
            self.ld(xin[:, off + s0:off + s0 + nv].rearrange("(k p) t -> p k t", p=128), xo2[:, :, 0:nv], W=[xin.tensor.name])
        sc.close()

    def phaseOut(self, xT):
        T = self.T
        sc = Scope(self)
        xi = [sc.sb("xoi%d" % i, [128, 8, 128]) for i in range(2)]
        og = [sc.sb("ostg%d" % i, [128, D]) for i in range(2)]
        for i in range(T // 128):
            t0 = TC + i * 128
            self.ld(xi[i % 2][:], xT[:, t0:t0 + 128].rearrange("(k p) t -> p k t", p=128))
            for half in range(2):
                p = self.pb[(2 * i + half) % 8]
                for q in range(4):
                    self.tr(p[:, q * 128:(q + 1) * 128], xi[i % 2][:, half * 4 + q, :], self.identF[:])
                self.cp("act" if half else "dve", og[i % 2][:, half * 512:(half + 1) * 512], p[:])
            self.ld(self.out_d[i * 128:(i + 1) * 128, :], og[i % 2][:], W=["out"])
        sc.close()


def rope_ab():
    ab = np.zeros((128, 4), np.float32)
    inv8 = 10000.0 ** (-np.arange(8) / 8.0)
    inv16 = 10000.0 ** (-np.arange(16) / 16.0)
    for i in range(32):
        j = i % 16
        ab[64 + i, 0 if j < 8 else 1] = inv8[j % 8] / TWO_PI
    for p in range(128):
        j = (p % 64) % 32
        ab[p, 2 if j < 16 else 3] = inv16[j % 16] / TWO_PI
    return ab


WEIGHT_KEYS = ["w_mod", "b_mod", "norm1", "w_in", "mla_q_lora_g", "mla_w_uq", "mla_kv_lora_g", "mla_w_ukv", "mla_q_norm",
               "mla_k_norm", "s5_a_re", "s5_a_im", "s5_log_dt", "s5_b_re", "s5_b_im", "s5_c_re", "s5_c_im", "s5_d",
               "s5_w_glu", "s5_b_glu", "swa_q_norm", "swa_k_norm", "swa_sink", "w_out", "norm2", "w_up", "conv_w",
               "conv_b", "w_down"]


def make_in_maps(inputs, n_cores, T):
    maps = []
    ab = rope_ab()
    for b in range(n_cores):
        m = {k: np.ascontiguousarray(inputs[k], dtype=np.float32) for k in WEIGHT_KEYS}
        m["x"] = np.ascontiguousarray(inputs["x"][b, :T], dtype=np.float32)
        m["ctx"] = np.ascontiguousarray(inputs["ctx"][b], dtype=np.float32)
        m["cvec"] = np.ascontiguousarray(np.stack([inputs["c"][b], inputs["c_ctx"]]), dtype=np.float32)
        m["rope_ab"] = ab
        maps.append(m)
    return maps


def kernel(**inputs):
    B, T = inputs["x"].shape[0], inputs["x"].shape[1]
    kb = KB(T, 4)
    nc = kb.build()
    maps = make_in_maps(inputs, B, T)
    res = run_bass_kernel_spmd(nc, maps, core_ids=list(range(B)))
    return np.stack([np.asarray(r["out"], dtype=np.float32) for r in res.results], axis=0)
```

```python
import math
import numpy as np
import concourse.bass as bass
import concourse.mybir as mybir
from concourse.bass_utils import run_bass_kernel_spmd

F32 = mybir.dt.float32
BF16 = mybir.dt.bfloat16
I32 = mybir.dt.int32
AF = mybir.ActivationFunctionType
ALU = mybir.AluOpType

D = 1024
TC = 256
EPS = 1e-6
MAGIC = 1.5 * 2 ** 23
TWO_PI = 2.0 * math.pi
IN_OFF = dict(cq=0, ckv=384, kr=640, u=672, qs=928, ks=1312, vs=1440)
DFF = 2816


class Buf:
    __slots__ = ("w", "r", "psum")

    def __init__(self, psum=False):
        self.w = None
        self.r = {}
        self.psum = psum


class FW:
    def __init__(self, nc, n_dma_sems=40):
        self.nc = nc
        self.engs = {"pe": nc.tensor, "act": nc.scalar, "dve": nc.vector, "pool": nc.gpsimd, "sp": nc.sync}
        self.sem, self.cnt = {}, {}
        self.seen = {k: {} for k in self.engs}
        self._ctx = []
        for k in ("pe", "act", "dve", "pool"):
            g = nc.semaphore("s_" + k)
            self.sem[k] = g.__enter__()
            self._ctx.append(g)
            self.cnt[k] = 0
        self.dsems = []
        for i in range(n_dma_sems):
            g = nc.semaphore("d_%d" % i)
            self.dsems.append(g.__enter__())
            self._ctx.append(g)
        self.dcount = [0] * n_dma_sems
        self.dnext = 0
        self.bufs = {}
        self.n_wait = 0
        self.n_inst = 0
        import os
        self.lim = int(os.environ.get("KLIM", "1000000000"))

    def close(self):
        for g in reversed(self._ctx):
            g.__exit__(None, None, None)

    def _key(self, x):
        if isinstance(x, (str, tuple)):
            k = x
        elif hasattr(x, "tensor"):
            k = x.tensor.name
        else:
            k = x.name
        b = self.bufs.get(k)
        if b is None:
            b = self.bufs[k] = Buf(isinstance(k, str) and k.startswith("pb"))
        return b

    def _wait(self, e, dep, same_ok=False):
        if dep is None:
            return
        kind, s, v = dep
        if kind == "eng" and s == e and (not same_ok or e == "pe"):
            return
        seen = self.seen[e]
        key = (kind, s)
        if seen.get(key, -1) >= v:
            return
        seen[key] = v
        self.engs[e].wait_ge(self.sem[s] if kind == "eng" else self.dsems[s], v)
        self.n_wait += 1

    def _deps(self, e, reads, writes):
        rb = [self._key(x) for x in reads]
        wb = [self._key(x) for x in writes]
        for b in rb:
            self._wait(e, b.w, True)
            if b.psum:
                for d in b.r.values():
                    self._wait(e, d)
        for b in wb:
            self._wait(e, b.w, True)
            for d in b.r.values():
                self._wait(e, d)
        return rb, wb

    def _commit(self, dep, rb, wb, rkey):
        for b in rb:
            b.r[rkey] = dep
        for b in wb:
            b.w = dep
            b.r = {}

    def op(self, e, fn, reads, writes):
        if self.n_inst >= self.lim:
            return
        rb, wb = self._deps(e, reads, writes)
        ins = fn()
        self.cnt[e] += 1
        ins.then_inc(self.sem[e], 1)
        self.n_inst += 1
        self._commit(("eng", e, self.cnt[e]), rb, wb, e)

    def dma(self, e, out, in_, reads, writes, **kw):
        if self.n_inst >= self.lim:
            return
        rb, wb = self._deps(e, reads, writes)
        i = self.dnext
        self.dnext = (self.dnext + 1) % len(self.dsems)
        if self.dcount[i]:
            self._wait(e, ("dma", i, self.dcount[i]))
        self.dcount[i] += 16
        self.engs[e].dma_start(out=out, in_=in_, **kw).then_inc(self.dsems[i], 16)
        self.n_inst += 1
        self._commit(("dma", i, self.dcount[i]), rb, wb, ("dma", i))

    def barrier(self):
        for e in self.engs:
            for o in ("pe", "act", "dve", "pool"):
                if o != e:
                    self._wait(e, ("eng", o, self.cnt[o]))
            for i in range(len(self.dsems)):
                if self.dcount[i]:
                    self._wait(e, ("dma", i, self.dcount[i]))
        self.bufs = {}


class Scope:
    def __init__(self, kb):
        self.kb = kb
        self.gs = []

    def sb(self, name, shape, dt=F32):
        self.kb.uid += 1
        g = self.kb.nc.sbuf_tensor("%s_%d" % (name, self.kb.uid), list(shape), dt)
        t = g.__enter__()
        self.gs.append(g)
        return t

    def close(self):
        self.kb.fw.barrier()
        for g in reversed(self.gs):
            g.__exit__(None, None, None)
        self.gs = []


class KB:
    def __init__(self, T, depth, dbg=(), l0=0):
        self.T, self.NT, self.depth, self.dbg = T, TC + T, depth, set(dbg)
        self.l0 = l0
        self.NTT = self.NT // 128
        nc = self.nc = bass.Bass("TRN2", target_bir_lowering=False)
        self.fw = FW(nc)
        self.uid = 0
        self.inp = {}
        self.glob = Scope(self)
        self.pb = []
        for i in range(8):
            g = nc.psum_tensor("pb%d" % i, [128, 512], F32)
            self.pb.append(g.__enter__())
            self.fw._ctx.append(g)
        self.chunks = [(0, TC)] + [(TC + 512 * i, 512) for i in range(T // 512)]

    def din(self, name, shape, dt=F32):
        a = self.nc.dram_tensor(name, list(shape), dt, kind="ExternalInput").ap()
        self.inp[name] = a
        return a

    def dscr(self, name, shape, dt=F32):
        kind = "ExternalOutput" if name in self.dbg else "Internal"
        return self.nc.dram_tensor(name, list(shape), dt, kind=kind).ap()

    def _rw(self, outs, ins, R, W):
        r = list(R) if R is not None else [x for x in ins if not isinstance(x, (int, float)) and x is not None]
        w = list(W) if W is not None else list(outs)
        return r, w

    def mm(self, out, lhsT, rhs, start=True, stop=True, R=None, W=None):
        r, w = self._rw([out], [lhsT, rhs], R, W)
        self.fw.op("pe", lambda: self.nc.tensor.matmul(out, lhsT=lhsT, rhs=rhs, start=start, stop=stop), r, w)

    def tr(self, out, in_, ident, R=None, W=None):
        r, w = self._rw([out], [in_, ident], R, W)
        self.fw.op("pe", lambda: self.nc.tensor.transpose(out, in_, ident), r, w)

    def act(self, out, in_, func, scale=1.0, bias=None, R=None, W=None):
        r, w = self._rw([out], [in_, scale, bias], R, W)
        kw = {}
        if bias is not None:
            kw["bias"] = bias
        self.fw.op("act", lambda: self.nc.scalar.activation(out=out, in_=in_, func=func, scale=scale, **kw), r, w)

    def ts(self, e, out, in0, s1, s2=None, op0=ALU.mult, op1=None, R=None, W=None):
        r, w = self._rw([out], [in0, s1, s2], R, W)
        eng = self.fw.engs[e]
        if op1 is None:
            f = lambda: eng.tensor_scalar(out=out, in0=in0, scalar1=s1, scalar2=None, op0=op0)
        else:
            f = lambda: eng.tensor_scalar(out=out, in0=in0, scalar1=s1, scalar2=s2, op0=op0, op1=op1)
        self.fw.op(e, f, r, w)

    def tt(self, e, out, in0, in1, op, R=None, W=None):
        r, w = self._rw([out], [in0, in1], R, W)
        eng = self.fw.engs[e]
        self.fw.op(e, lambda: eng.tensor_tensor(out=out, in0=in0, in1=in1, op=op), r, w)

    def stt(self, out, in0, scalar, in1, op0, op1, R=None, W=None):
        r, w = self._rw([out], [in0, scalar, in1], R, W)
        self.fw.op("dve", lambda: self.nc.vector.scalar_tensor_tensor(out=out, in0=in0, scalar=scalar, in1=in1, op0=op0, op1=op1), r, w)

    def cp(self, e, out, in_, R=None, W=None):
        r, w = self._rw([out], [in_], R, W)
        if e == "act":
            f = lambda: self.nc.scalar.copy(out=out, in_=in_)
        else:
            eng = self.fw.engs[e]
            f = lambda: eng.tensor_copy(out=out, in_=in_)
        self.fw.op(e, f, r, w)

    def recip(self, out, in_, R=None, W=None):
        r, w = self._rw([out], [in_], R, W)
        self.fw.op("dve", lambda: self.nc.vector.reciprocal(out=out, in_=in_), r, w)

    def memset(self, e, out, val, R=None, W=None):
        r, w = self._rw([out], [], R, W)
        eng = self.fw.engs[e]
        self.fw.op(e, lambda: eng.memset(out, val), r, w)

    def scan(self, out, d0, d1, init, R=None, W=None):
        r, w = self._rw([out], [d0, d1, init], R, W)
        self.fw.op("dve", lambda: self.nc.vector.tensor_tensor_scan(out=out, data0=d0, data1=d1, initial=init, op0=ALU.mult, op1=ALU.add), r, w)

    def ld(self, out, in_, R=None, W=None, e="sp", **kw):
        r, w = self._rw([out], [in_], R, W)
        self.fw.dma(e, out, in_, r, w, **kw)

    def ld_T(self, sc, dst, src_rows, n, pbank):
        stg = sc.sb("ldT", [128, 128])
        self.ld(stg[0:n, :], src_rows)
        self.tr(pbank[:, 0:n], stg[0:n, :], self.identF[0:n, 0:n])
        self.cp("dve", dst, pbank[:, 0:n])

    def rstd(self, out, in_, scale, bias_ap):
        self.act(out, in_, AF.Sqrt, scale=scale, bias=bias_ap)
        self.recip(out, out)

    def build(self):
        nc, T, NT = self.nc, self.T, self.NT
        L = 4
        i_ = self.din
        self.x_d = i_("x", [T, D])
        self.ctx_d = i_("ctx", [TC, D])
        self.cvec_d = i_("cvec", [2, D])
        self.ropeab_d = i_("rope_ab", [128, 4])
        self.w = {}
        G, PS, HG = 16, 64, 16
        shapes = dict(w_mod=[L, D, 6 * D], b_mod=[L, 6 * D], norm1=[L, D], w_in=[L, D, 1568], mla_q_lora_g=[L, 384],
                      mla_w_uq=[L, 384, 576], mla_kv_lora_g=[L, 256], mla_w_ukv=[L, 256, 768], mla_q_norm=[L, 96],
                      mla_k_norm=[L, 96], s5_a_re=[L, 2, G, PS], s5_a_im=[L, 2, G, PS], s5_log_dt=[L, 2, G],
                      s5_b_re=[L, 2, G, PS, HG], s5_b_im=[L, 2, G, PS, HG], s5_c_re=[L, 2, G, HG, PS],
                      s5_c_im=[L, 2, G, HG, PS], s5_d=[L, 256], s5_w_glu=[L, 256, 256], s5_b_glu=[L, 256],
                      swa_q_norm=[L, 64], swa_k_norm=[L, 64], swa_sink=[L, 6], w_out=[L, D, D], norm2=[L, D],
                      w_up=[L, D, 2 * DFF], conv_w=[L, 3, 2 * DFF], conv_b=[L, 2 * DFF], w_down=[L, DFF, D])
        for k, s in shapes.items():
            self.w[k] = i_(k, s)
        self.out_d = nc.dram_tensor("out", [T, D], F32, kind="ExternalOutput").ap()
        s_ = self.dscr
        self.xT = [s_("xTa", [D, NT]), s_("xTb", [D, NT])]
        self.qT = s_("qT", [6, 96, NT], BF16)
        self.kT = s_("kT", [6, 96, NT], BF16)
        self.vA = s_("vA", [6, 128, NT // 128, 64], BF16)
        self.qsT = s_("qsT", [6, 64, NT], BF16)
        self.ksT = s_("ksT", [128, NT], BF16)
        self.vS = s_("vS", [2, 128, NT // 128, 64], BF16)
        self.uTb = s_("uTb", [256, NT], BF16)
        self.uT32 = s_("uT32", [256, NT])
        self.ysum = s_("ysum", [256, NT])
        self.ycT = s_("ycT", [D, NT], BF16)
        self.ropeM = [s_("ropeMC", [96, NT]), s_("ropeMS", [96, NT])]
        self.ropeS = [s_("ropeSC", [128, NT]), s_("ropeSS", [128, NT])]
        self._ncd = nc.allow_non_contiguous_dma(reason="small strided parameter loads")
        self._ncd.__enter__()
        self.phase0()
        cur = 0
        for l in range(self.l0, self.l0 + self.depth):
            last = l == self.l0 + self.depth - 1
            if "stop0" in self.dbg:
                break
            self.phaseM(l)
            if "stopM" in self.dbg:
                break
            self.phaseA(l, self.xT[cur])
            if "stopA" in self.dbg:
                break
            self.phaseB(l, last and "stopC" not in self.dbg)
            self.phaseC(l, last and "stopC" not in self.dbg)
            if "stopC" in self.dbg and "stopD" not in self.dbg:
                break
            self.phaseD(l, last)
            if "stopD" in self.dbg:
                break
            self.phaseE(l, self.xT[cur], self.xT[1 - cur], last)
            if "stopL1" in self.dbg:
                break
        if not ({"stopA", "stop0", "stopM", "stopC", "stopD", "stopL1"} & self.dbg):
            self.phaseOut(self.xT[cur])
        self.fw.barrier()
        self._ncd.__exit__(None, None, None)
        return nc

    def phase0(self):
        nc, T, NT = self.nc, self.T, self.NT
        g = self.glob
        self.identF = g.sb("identF", [128, 128])
        self.onesB = g.sb("onesB", [128, 128], BF16)
        self.onesBD = g.sb("onesBD", [128, 128], BF16)
        self.mlo = g.sb("mlo", [128, 128], BF16)
        self.mhi = g.sb("mhi", [128, 128], BF16)
        self.cst = g.sb("cst", [128, 8])
        for j, v in enumerate([EPS, 96 * EPS, 64 * EPS, math.pi / 2, 0.0]):
            self.memset("dve", self.cst[:, j:j + 1], v)
        self.memset("pool", self.identF[:], 1.0)
        self.fw.op("pool", lambda: nc.gpsimd.affine_select(out=self.identF[:], in_=self.identF[:], compare_op=ALU.is_equal,
                                                           fill=0.0, base=0, pattern=[[-1, 128]], channel_multiplier=1),
                   [self.identF], [self.identF])
        self.memset("dve", self.onesB[:], 1.0)
        self.memset("dve", self.onesBD[:], 0.0)
        self.memset("dve", self.onesBD[0:64, 0:64], 1.0)
        self.memset("dve", self.onesBD[64:128, 64:128], 1.0)
        self.memset("pool", self.mlo[:], 1.0)
        self.memset("pool", self.mhi[:], 1.0)
        self.fw.op("pool", lambda: nc.gpsimd.affine_select(out=self.mlo[:], in_=self.mlo[:], compare_op=ALU.is_ge, fill=0.0,
                                                           base=0, pattern=[[-1, 128]], channel_multiplier=1),
                   [self.mlo], [self.mlo])
        self.fw.op("pool", lambda: nc.gpsimd.affine_select(out=self.mhi[:], in_=self.mhi[:], compare_op=ALU.is_ge, fill=0.0,
                                                           base=0, pattern=[[1, 128]], channel_multiplier=-1),
                   [self.mhi], [self.mhi])
        sc = Scope(self)
        ab = sc.sb("ab", [128, 4])
        self.ld(ab[:], self.ropeab_d)
        RC = min(2048, T)
        rowi = sc.sb("rowi", [128, RC], I32)
        coli = sc.sb("coli", [128, RC], I32)
        rowf = sc.sb("rowf", [128, RC])
        colf = sc.sb("colf", [128, RC])
        self.fw.op("pool", lambda: nc.gpsimd.iota(rowi[:], pattern=[[1, RC // 64], [0, 64]], base=0, channel_multiplier=0), [], [rowi])
        self.fw.op("pool", lambda: nc.gpsimd.iota(coli[:], pattern=[[0, RC // 64], [1, 64]], base=0, channel_multiplier=0), [], [coli])
        self.cp("dve", colf[:], coli[:])
        rowf0 = sc.sb("rowf0", [128, RC])
        self.cp("dve", rowf0[:], rowi[:])
        one = sc.sb("one", [128, TC])
        zero = sc.sb("zero", [128, TC])
        self.memset("dve", one[:], 1.0)
        self.memset("dve", zero[:], 0.0)
        ang = sc.sb("ang", [128, RC])
        tmp = sc.sb("tmpa", [128, RC])
        cs = sc.sb("cs", [128, RC])
        sn = sc.sb("sn", [128, RC])
        for (tabs, npart, ca, cb, nonrope) in ((self.ropeM, 96, 0, 1, 64), (self.ropeS, 128, 2, 3, 0)):
            self.ld(tabs[0][:, 0:TC], one[0:npart, :], W=[tabs[0].tensor.name])
            self.ld(tabs[1][:, 0:TC], zero[0:npart, :], W=[tabs[1].tensor.name])
            for c0 in range(0, T, RC):
                self.ts("dve", rowf[:], rowf0[:], float(c0 // 64), None, op0=ALU.add)
                P_ = slice(0, npart)
                self.ts("dve", ang[P_], rowf[P_], ab[P_, ca:ca + 1], None, op0=ALU.mult)
                self.stt(ang[P_], colf[P_], ab[P_, cb:cb + 1], ang[P_], ALU.mult, ALU.add)
                self.ts("dve", tmp[P_], ang[P_], MAGIC, None, op0=ALU.add)
                self.stt(tmp[P_], tmp[P_], -MAGIC, ang[P_], ALU.add, ALU.subtract)
                self.act(sn[P_], tmp[P_], AF.Sin, scale=-TWO_PI)
                self.stt(tmp[P_], tmp[P_], -1.0, tmp[P_], ALU.mult, ALU.max)
                self.act(cs[P_], tmp[P_], AF.Sin, scale=-TWO_PI, bias=self.cst[P_, 3:4])
                if nonrope:
                    self.memset("dve", cs[0:nonrope, :], 1.0)
                    self.memset("dve", sn[0:nonrope, :], 0.0)
                self.ld(tabs[0][:, TC + c0:TC + c0 + RC], cs[P_, :], W=[tabs[0].tensor.name])
                self.ld(tabs[1][:, TC + c0:TC + c0 + RC], sn[P_, :], W=[tabs[1].tensor.name])
        xT = self.xT[0]
        tiles = [(self.ctx_d, i * 128, i * 128) for i in range(TC // 128)] + [(self.x_d, i * 128, TC + i * 128) for i in range(T // 128)]
        stgs = [sc.sb("xstg0", [128, 8, 512]), sc.sb("xstg1", [128, 8, 512])]
        xins = [sc.sb("xin%d" % i, [128, D]) for i in range(3)]
        for g0 in range(0, len(tiles), 4):
            grp = tiles[g0:g0 + 4]
            stg = stgs[(g0 // 4) % 2]
            for j, (src, r0, t0) in enumerate(grp):
                xin = xins[(g0 + j) % 3]
                self.ld(xin[:], src[r0:r0 + 128, :])
                for half in range(2):
                    pb = self.pb[(2 * (g0 + j) + half) % 8]
                    for q in range(4):
                        k = half * 4 + q
                        self.tr(pb[:, q * 128:(q + 1) * 128], xin[:, k * 128:(k + 1) * 128], self.identF[:])
                    dst = stg[:, half * 4:half * 4 + 4, j * 128:(j + 1) * 128]
                    srcp = pb[:].rearrange("p (q t) -> p q t", q=4)
                    self.cp("act" if half else "dve", dst, srcp, R=[pb], W=[stg])
            n = 128 * len(grp)
            t0 = grp[0][2]
            self.ld(xT[:, t0:t0 + n].rearrange("(k p) t -> p k t", p=128), stg[:, :, 0:n], W=[xT.tensor.name])
        sc.close()

    def phaseM(self, l):
        nc = self.nc
        if not hasattr(self, "mod"):
            g = self.glob
            self.mod = g.sb("mod", [128, 48, 2])
            self.AB = g.sb("AB", [128, 4, 8, 2])
            self.sT = g.sb("sT", [128, 8, 2])
            cv = g.sb("cv", [128, 2, 8])
            self.ld_T(g, cv[:].rearrange("p j k -> p (j k)"), self.cvec_d.rearrange("j (k p) -> (j k) p", p=128), 16, self.pb[1])
            self.act(self.sT[:], cv[:].rearrange("p j k -> p k j"), AF.Silu)
        sc = Scope(self)
        bT = sc.sb("bT", [128, 48])
        self.ld_T(sc, bT[:], self.w["b_mod"][l].rearrange("(f p) -> f p", p=128), 48, self.pb[1])
        nrm = sc.sb("nrm", [128, 2, 8])
        self.ld_T(sc, nrm[:, 0, :], self.w["norm1"][l].rearrange("(k p) -> k p", p=128), 8, self.pb[2])
        self.ld_T(sc, nrm[:, 1, :], self.w["norm2"][l].rearrange("(k p) -> k p", p=128), 8, self.pb[3])
        pacc = self.pb[0]
        wst = [sc.sb("wst0", [128, 8, 768]), sc.sb("wst1", [128, 8, 768])]
        for blk in range(8):
            ws = wst[blk % 2]
            self.ld(ws[:], self.w["w_mod"][l][:, blk * 768:(blk + 1) * 768].rearrange("(k p) n -> p k n", p=128))
            for f in range(6):
                ft = blk * 6 + f
                for k in range(8):
                    self.mm(pacc[:, 2 * ft:2 * ft + 2], ws[:, k, f * 128:(f + 1) * 128], self.sT[:, k, :], start=(k == 0), stop=(k == 7))
        pv = pacc[:, 0:96].rearrange("p (f c) -> p f c", c=2)
        for c in range(2):
            self.tt("dve", self.mod[:, :, c], pv[:, :, c], bT[:], ALU.add, R=[pacc, bT], W=[self.mod])
        for j, (sh_i, sc_i) in enumerate(((0, 1), (3, 4))):
            for c in range(2):
                self.stt(self.AB[:, 2 * j, :, c], self.mod[:, sc_i * 8:(sc_i + 1) * 8, c], 1.0, nrm[:, j, :], ALU.add, ALU.mult,
                         R=[self.mod, nrm], W=[self.AB])
                self.cp("dve", self.AB[:, 2 * j + 1, :, c], self.mod[:, sh_i * 8:(sh_i + 1) * 8, c], R=[self.mod], W=[self.AB])
        sc.close()

    def phaseA(self, l, xT):
        nc, T, NT = self.nc, self.T, self.NT
        W = self.w
        sc = Scope(self)
        win = sc.sb("win", [128, 8, 1568], BF16)
        for k in range(8):
            self.ld(win[:, k, :], W["w_in"][l][k * 128:(k + 1) * 128, :], e="pool", W=[win])
        winr = sc.sb("winr", [128, 8, 544], BF16)
        def rot(dst, src, nh, half):
            d = dst.rearrange("p k (h two x) -> p k h two x", h=nh, two=2, x=half)
            s = src.rearrange("p k (h two x) -> p k h two x", h=nh, two=2, x=half)
            for k in range(8):
                self.ts("dve", d[:, k, :, 0, :], s[:, k, :, 1, :], -1.0, None, op0=ALU.mult, R=[win], W=[winr])
                self.cp("dve", d[:, k, :, 1, :], s[:, k, :, 0, :], R=[win], W=[winr])
        rot(winr[:, :, 0:32], win[:, :, 640:672], 1, 16)
        rot(winr[:, :, 32:416], win[:, :, 928:1312], 6, 32)
        rot(winr[:, :, 416:544], win[:, :, 1312:1440], 2, 32)
        wuq = sc.sb("wuq", [128, 3, 576], BF16)
        self.ld(wuq[:], W["mla_w_uq"][l].rearrange("(k p) n -> p k n", p=128), e="pool")
        wuqr = sc.sb("wuqr", [128, 3, 6, 96], BF16)
        self.memset("dve", wuqr[:], 0.0)
        wq5 = wuq[:].rearrange("p k (h d) -> p k h d", h=6)
        for k in range(3):
            self.ts("dve", wuqr[:, k, :, 64:80], wq5[:, k, :, 80:96], -1.0, None, op0=ALU.mult, R=[wuq], W=[wuqr])
            self.cp("dve", wuqr[:, k, :, 80:96], wq5[:, k, :, 64:80], R=[wuq], W=[wuqr])
        wukv = sc.sb("wukv", [128, 2, 768], BF16)
        self.ld(wukv[:], W["mla_w_ukv"][l].rearrange("(k p) n -> p k n", p=128), e="pool")
        gv = sc.sb("gv", [128, 16])
        self.memset("dve", gv[:], 0.0)
        col = lambda v: v.rearrange("(p o) -> p o", o=1)
        self.ld(gv[:, 0:3], W["mla_q_lora_g"][l].rearrange("(k p) -> p k", p=128), W=[gv])
        self.ld(gv[:, 3:5], W["mla_kv_lora_g"][l].rearrange("(k p) -> p k", p=128), W=[gv])
        qn, kn = W["mla_q_norm"][l], W["mla_k_norm"][l]
        self.ld(gv[0:96, 5:6], col(qn), W=[gv])
        self.ld(gv[64:80, 6:7], col(qn[80:96]), W=[gv])
        self.ld(gv[80:96, 6:7], col(qn[64:80]), W=[gv])
        self.ld(gv[0:64, 7:8], col(kn[0:64]), W=[gv])
        self.ld(gv[64:96, 8:9], col(kn[64:96]), W=[gv])
        self.ld(gv[64:80, 9:10], col(kn[80:96]), W=[gv])
        self.ld(gv[80:96, 9:10], col(kn[64:80]), W=[gv])
        sq_, sk_ = W["swa_q_norm"][l], W["swa_k_norm"][l]
        for hh in range(2):
            b0 = hh * 64
            self.ld(gv[b0:b0 + 64, 10:11], col(sq_), W=[gv])
            self.ld(gv[b0:b0 + 32, 11:12], col(sq_[32:64]), W=[gv])
            self.ld(gv[b0 + 32:b0 + 64, 11:12], col(sq_[0:32]), W=[gv])
            self.ld(gv[b0:b0 + 64, 12:13], col(sk_), W=[gv])
            self.ld(gv[b0:b0 + 32, 13:14], col(sk_[32:64]), W=[gv])
            self.ld(gv[b0 + 32:b0 + 64, 13:14], col(sk_[0:32]), W=[gv])
        cst = self.cst
        NB = 2
        xs = [sc.sb("xs%d" % i, [128, 8, 512]) for i in range(NB)]
        sq = sc.sb("sq", [128, 8, 512], BF16)
        rs = sc.sb("rs", [128, 512])
        tmp = [sc.sb("tmp%d" % i, [128, 512]) for i in range(2)]
        h = sc.sb("h", [128, 8, 512], BF16)
        cqf = sc.sb("cqf", [128, 5, 512])
        sqc = sc.sb("sqc", [128, 5, 512], BF16)
        rsc = sc.sb("rsc", [128, 2, 512])
        cn = sc.sb("cn", [128, 5, 512], BF16)
        tabs = sc.sb("tabs", [128, 4, 512])
        sqh = [sc.sb("sqh%d" % i, [128, 512], BF16) for i in range(2)]
        rq = [sc.sb("rq%d" % i, [128, 512]) for i in range(2)]
        e1 = [sc.sb("e1_%d" % i, [128, 512]) for i in range(2)]
        e2 = [sc.sb("e2_%d" % i, [128, 512]) for i in range(2)]
        qst = sc.sb("qst", [96, 6, 512], BF16)
        kst = sc.sb("kst", [96, 6, 512], BF16)
        krr = sc.sb("krr", [128, 512])
        sqr = sc.sb("sqr", [128, 512], BF16)
        vst = sc.sb("vst", [128, 6, 4, 64], BF16)
        ust = sc.sb("ust", [128, 2, 512])
        usb = sc.sb("usb", [128, 2, 512], BF16)
        qsst = sc.sb("qsst", [128, 3, 512], BF16)
        ksst = sc.sb("ksst", [128, 512], BF16)
        vsst = sc.sb("vsst", [128, 2, 4, 64], BF16)
        pb = self.pb
        pst = [pb[0], pb[1]]
        pm = [pb[2], pb[3], pb[4]]
        pr = [pb[5], pb[6]]
        pv = pb[7]
        cnt = dict(st=0, m=0, r=0, e=0)

        def nxt(key, arr):
            cnt[key] += 1
            return arr[cnt[key] % len(arr)]

        import os
        CUT = int(os.environ.get("KCUT", "99"))
        if CUT <= 1:
            sc.close()
            return
        for ci, (t0, n) in enumerate(self.chunks):
            col_ = 1 if ci == 0 else 0
            x_ = xs[ci % NB]
            self.ld(x_[:, :, 0:n], xT[:, t0:t0 + n].rearrange("(k p) t -> p k t", p=128))
            self.ld(tabs[0:96, 0, 0:n], self.ropeM[0][:, t0:t0 + n], W=[tabs])
            self.ld(tabs[0:96, 1, 0:n], self.ropeM[1][:, t0:t0 + n], W=[tabs])
            self.ld(tabs[:, 2, 0:n], self.ropeS[0][:, t0:t0 + n], W=[tabs])
            self.ld(tabs[:, 3, 0:n], self.ropeS[1][:, t0:t0 + n], W=[tabs])
            for k in range(8):
                self.act(sq[:, k, 0:n], x_[:, k, 0:n], AF.Square)
            p = nxt("st", pst)
            for k in range(8):
                self.mm(p[:, 0:n], self.onesB[:], sq[:, k, 0:n], start=(k == 0), stop=(k == 7))
            self.rstd(rs[:, 0:n], p[:, 0:n], 1.0 / D, cst[:, 0:1])
            for k in range(8):
                t_ = tmp[k % 2]
                self.stt(t_[:, 0:n], x_[:, k, 0:n], self.AB[:, 0, k, col_:col_ + 1], rs[:, 0:n], ALU.mult, ALU.mult)
                self.act(h[:, k, 0:n], t_[:, 0:n], AF.Identity, bias=self.AB[:, 1, k, col_:col_ + 1])

            if CUT <= 2:
                continue
            def proj(dst, wt, c0, m):
                for k in range(8):
                    self.mm(dst[0:m, 0:n], wt[:, k, c0:c0 + m], h[:, k, 0:n], start=(k == 0), stop=(k == 7))

            for m in range(5):
                p = nxt("m", pm)
                proj(p, win, m * 128, 128)
                self.act(sqc[:, m, 0:n], p[:, 0:n], AF.Square)
                self.cp("dve", cqf[:, m, 0:n], p[:, 0:n])
            for (j, ms, dim) in ((0, (0, 1, 2), 384), (1, (3, 4), 256)):
                p = nxt("st", pst)
                for i, m in enumerate(ms):
                    self.mm(p[:, 0:n], self.onesB[:], sqc[:, m, 0:n], start=(i == 0), stop=(i == len(ms) - 1))
                self.rstd(rsc[:, j, 0:n], p[:, 0:n], 1.0 / dim, cst[:, 0:1])
                for m in ms:
                    self.stt(cn[:, m, 0:n], cqf[:, m, 0:n], gv[:, m:m + 1], rsc[:, j, 0:n], ALU.mult, ALU.mult)
            if CUT <= 3:
                continue
            for hd in range(6):
                p = nxt("m", pm)
                r_ = nxt("r", pr)
                for k in range(3):
                    self.mm(p[0:96, 0:n], wuq[:, k, hd * 96:(hd + 1) * 96], cn[:, k, 0:n], start=(k == 0), stop=(k == 2))
                for k in range(3):
                    self.mm(r_[0:96, 0:n], wuqr[:, k, hd, :], cn[:, k, 0:n], start=(k == 0), stop=(k == 2))
                s_ = nxt("e", sqh)
                q_ = rq[cnt["e"] % 2]
                a_ = e1[cnt["e"] % 2]
                b_ = e2[cnt["e"] % 2]
                self.act(s_[0:96, 0:n], p[0:96, 0:n], AF.Square)
                ps_ = nxt("st", pst)
                self.mm(ps_[0:96, 0:n], self.onesB[0:96, 0:96], s_[0:96, 0:n])
                self.rstd(q_[0:96, 0:n], ps_[0:96, 0:n], 1.0, cst[0:96, 1:2])
                self.stt(a_[0:96, 0:n], p[0:96, 0:n], gv[0:96, 5:6], q_[0:96, 0:n], ALU.mult, ALU.mult)
                self.stt(b_[0:96, 0:n], r_[0:96, 0:n], gv[0:96, 6:7], q_[0:96, 0:n], ALU.mult, ALU.mult)
                self.tt("pool", a_[0:96, 0:n], a_[0:96, 0:n], tabs[0:96, 0, 0:n], ALU.mult)
                self.tt("pool", b_[0:96, 0:n], b_[0:96, 0:n], tabs[0:96, 1, 0:n], ALU.mult)
                self.tt("pool", qst[:, hd, 0:n], a_[0:96, 0:n], b_[0:96, 0:n], ALU.add)
            self.ld(self.qT[:, :, t0:t0 + n].rearrange("h p t -> p h t"), qst[:, :, 0:n], W=["qT"])
            if CUT <= 4:
                continue
            p = nxt("m", pm)
            r_ = nxt("r", pr)
            proj(p, win, IN_OFF["kr"], 32)
            proj(r_, winr, 0, 32)
            a_ = e1[0]
            b_ = e2[0]
            self.cp("act", a_[64:96, 0:n], p[0:32, 0:n])
            self.cp("dve", b_[64:96, 0:n], r_[0:32, 0:n])
            self.act(sqr[64:96, 0:n], a_[64:96, 0:n], AF.Square)
            self.ts("dve", a_[64:96, 0:n], a_[64:96, 0:n], gv[64:96, 8:9], None, op0=ALU.mult)
            self.ts("dve", b_[64:96, 0:n], b_[64:96, 0:n], gv[64:96, 9:10], None, op0=ALU.mult)
            self.tt("pool", a_[64:96, 0:n], a_[64:96, 0:n], tabs[64:96, 0, 0:n], ALU.mult)
            self.tt("pool", b_[64:96, 0:n], b_[64:96, 0:n], tabs[64:96, 1, 0:n], ALU.mult)
            self.tt("pool", krr[64:96, 0:n], a_[64:96, 0:n], b_[64:96, 0:n], ALU.add)
            for hd in range(6):
                p = nxt("m", pm)
                for k in range(2):
                    self.mm(p[0:64, 0:n], wukv[:, k, hd * 128:hd * 128 + 64], cn[:, 3 + k, 0:n], start=(k == 0), stop=(k == 1))
                s_ = nxt("e", sqh)
                q_ = rq[cnt["e"] % 2]
                self.act(s_[0:64, 0:n], p[0:64, 0:n], AF.Square)
                ps_ = nxt("st", pst)
                self.mm(ps_[0:96, 0:n], self.onesB[0:64, 0:96], s_[0:64, 0:n], start=True, stop=False)
                self.mm(ps_[0:96, 0:n], self.onesB[64:96, 0:96], sqr[64:96, 0:n], start=False, stop=True)
                self.rstd(q_[0:96, 0:n], ps_[0:96, 0:n], 1.0 / 96, cst[0:96, 0:1])
                self.stt(kst[0:64, hd, 0:n], p[0:64, 0:n], gv[0:64, 7:8], q_[0:64, 0:n], ALU.mult, ALU.mult)
                self.tt("pool", kst[64:96, hd, 0:n], krr[64:96, 0:n], q_[64:96, 0:n], ALU.mult)
            self.ld(self.kT[:, :, t0:t0 + n].rearrange("h p t -> p h t"), kst[:, :, 0:n], W=["kT"])
            if CUT <= 5:
                continue
            wv = wukv[:].rearrange("p k (h d) -> p k h d", h=6)
            for s in range(n // 128):
                pvv = pv[:, 0:384].rearrange("p (h d) -> p h d", h=6)
                for k in range(2):
                    self.mm(pvv, cn[:, 3 + k, s * 128:(s + 1) * 128], wv[:, k, :, 64:128], start=(k == 0), stop=(k == 1), W=[pv])
                self.cp("act", vst[:, :, s, :], pvv, R=[pv], W=[vst])
            self.ld(self.vA[:, :, t0 // 128:(t0 + n) // 128, :].rearrange("h p s d -> p h s d"), vst[:, :, 0:n // 128, :], W=["vA"])
            if CUT <= 6:
                continue
            for m in range(2):
                p = nxt("m", pm)
                proj(p, win, IN_OFF["u"] + m * 128, 128)
                self.cp("act", ust[:, m, 0:n], p[:, 0:n])
                self.cp("dve", usb[:, m, 0:n], p[:, 0:n])
            self.ld(self.uT32[:, t0:t0 + n].rearrange("(m p) t -> p m t", p=128), ust[:, :, 0:n], W=["uT32"])
            self.ld(self.uTb[:, t0:t0 + n].rearrange("(m p) t -> p m t", p=128), usb[:, :, 0:n], W=["uTb"])
            if CUT <= 7:
                continue
            for m in range(4):
                isq = m < 3
                p = nxt("m", pm)
                r_ = nxt("r", pr)
                proj(p, win, (IN_OFF["qs"] + m * 128) if isq else IN_OFF["ks"], 128)
                proj(r_, winr, (32 + m * 128) if isq else 416, 128)
                s_ = nxt("e", sqh)
                q_ = rq[cnt["e"] % 2]
                a_ = e1[cnt["e"] % 2]
                b_ = e2[cnt["e"] % 2]
                self.act(s_[:, 0:n], p[:, 0:n], AF.Square)
                ps_ = nxt("st", pst)
                self.mm(ps_[:, 0:n], self.onesBD[:], s_[:, 0:n])
                if isq:
                    self.rstd(q_[:, 0:n], ps_[:, 0:n], 1.0, cst[:, 2:3])
                else:
                    self.rstd(q_[:, 0:n], ps_[:, 0:n], 1.0 / 64, cst[:, 0:1])
                gc = 10 if isq else 12
                self.stt(a_[:, 0:n], p[:, 0:n], gv[:, gc:gc + 1], q_[:, 0:n], ALU.mult, ALU.mult)
                self.stt(b_[:, 0:n], r_[:, 0:n], gv[:, gc + 1:gc + 2], q_[:, 0:n], ALU.mult, ALU.mult)
                self.tt("pool", a_[:, 0:n], a_[:, 0:n], tabs[:, 2, 0:n], ALU.mult)
                self.tt("pool", b_[:, 0:n], b_[:, 0:n], tabs[:, 3, 0:n], ALU.mult)
                dst = qsst[:, m, 0:n] if isq else ksst[:, 0:n]
                self.tt("pool", dst, a_[:, 0:n], b_[:, 0:n], ALU.add)
            self.ld(self.qsT[:, :, t0:t0 + n].rearrange("(m two) d t -> (two d) m t", two=2), qsst[:, :, 0:n], W=["qsT"])
            self.ld(self.ksT[:, t0:t0 + n], ksst[:, 0:n], W=["ksT"])
            if CUT <= 8:
                continue
            for s in range(n // 128):
                for k in range(8):
                    self.mm(pv[:, 384:512], h[:, k, s * 128:(s + 1) * 128], win[:, k, 1440:1568], start=(k == 0), stop=(k == 7))
                self.cp("dve", vsst[:, :, s, :], pv[:, 384:512].rearrange("p (g d) -> p g d", g=2), R=[pv], W=[vsst])
            self.ld(self.vS[:, :, t0 // 128:(t0 + n) // 128, :].rearrange("g p s d -> p g s d"), vsst[:, :, 0:n // 128, :], W=["vS"])
        sc.close()

    def pipeline(self, items, sk=5):
        n = len(items)
        for i in range(n + sk):
            if i < n:
                if items[i][0]:
                    items[i][0]()
                items[i][1]()
            if i >= sk:
                items[i - sk][2]()
                if items[i - sk][3]:
                    items[i - sk][3]()

    def phaseB(self, l, last):
        NT, NTT = self.NT, self.NTT
        sc = Scope(self)
        kh = [sc.sb("kh%d" % i, [96, NT], BF16) for i in range(2)]
        vh = [sc.sb("vh%d" % i, [128, NTT, 128], BF16) for i in range(2)]
        vstg = [sc.sb("vstg%d" % i, [128, NTT, 64], BF16) for i in range(2)]
        for v in vh:
            self.memset("dve", v[:, :, 64:128], 1.0)
        qc = [sc.sb("qc%d" % i, [96, 512], BF16) for i in range(3)]
        pT = [sc.sb("pT%d" % i, [128, 512], BF16) for i in range(6)]
        yo = [sc.sb("yo%d" % i, [64, 512], BF16) for i in range(2)]
        rd = [sc.sb("rd%d" % i, [64, 512]) for i in range(2)]
        S = self.pb[0:4] + self.pb[6:8]
        acc = self.pb[4:6]
        chunks = self.chunks[1:] if last else self.chunks
        groups = [(hd, ci) for hd in range(6) for ci in range(len(chunks))]

        def ld_head(hd):
            self.ld(kh[hd % 2][:], self.kT[hd])
            self.ld(vstg[hd % 2][:], self.vA[hd])
            self.cp("pool", vh[hd % 2][:, :, 0:64], vstg[hd % 2][:])

        def ld_q(gi):
            hd, ci = groups[gi]
            t0, n = chunks[ci]
            self.ld(qc[gi % 3][:, 0:n], self.qT[hd, :, t0:t0 + n])

        items = []
        it = 0
        for gi, (hd, ci) in enumerate(groups):
            t0, n = chunks[ci]
            kts = list(range(TC // 128)) if t0 < TC else list(range(NTT))
            for j, kt in enumerate(kts):
                pre = None
                if j == 0:
                    def pre(gi=gi, hd=hd, ci=ci):
                        if gi + 1 < len(groups):
                            ld_q(gi + 1)
                if j == 7 and ci == 1:
                    def pre(gi=gi, hd=hd, ci=ci):
                        if hd + 1 < 6:
                            ld_head(hd + 1)

                def s1(it=it, gi=gi, hd=hd, kt=kt, n=n):
                    self.mm(S[it % 6][:, 0:n], kh[hd % 2][:, kt * 128:(kt + 1) * 128], qc[gi % 3][:, 0:n])
                    self.act(pT[it % 6][:, 0:n], S[it % 6][:, 0:n], AF.Exp)

                def s2(it=it, gi=gi, hd=hd, kt=kt, n=n, first=(j == 0), lastk=(j == len(kts) - 1)):
                    self.mm(acc[gi % 2][:, 0:n], vh[hd % 2][:, kt, :], pT[it % 6][:, 0:n], start=first, stop=lastk)

                fin = None
                if j == len(kts) - 1:
                    def fin(gi=gi, hd=hd, t0=t0, n=n):
                        a, r_, y_ = acc[gi % 2], rd[gi % 2], yo[gi % 2]
                        self.cp("act", r_[:, 0:n], a[64:128, 0:n])
                        self.recip(r_[:, 0:n], r_[:, 0:n])
                        self.tt("dve", y_[:, 0:n], a[0:64, 0:n], r_[:, 0:n], ALU.mult)
                        self.ld(self.ycT[hd * 64:(hd + 1) * 64, t0:t0 + n], y_[:, 0:n], W=["ycT"])
                items.append((pre, s1, s2, fin))
                it += 1
        ld_head(0)
        ld_q(0)
        self.pipeline(items)
        sc.close()

    def phaseC(self, l, last):
        NT, NTT, T = self.NT, self.NTT, self.T
        sc = Scope(self)
        ks = sc.sb("ks", [128, NT], BF16)
        self.ld(ks[:], self.ksT)
        vs = sc.sb("vs", [128, 2, NTT, 128], BF16)
        vsg = sc.sb("vsg", [128, 2, NTT, 64], BF16)
        self.memset("dve", vs[:, :, :, 64:128], 1.0)
        for g in range(2):
            self.ld(vsg[:, g], self.vS[g], W=[vsg])
            self.cp("pool", vs[:, g, :, 0:64], vsg[:, g])
        m3 = [sc.sb("mlo3", [128, 3, 128], BF16), sc.sb("mhi3", [128, 3, 128], BF16)]
        for j in range(3):
            self.cp("dve", m3[0][:, j, :], self.mlo[:])
            self.cp("dve", m3[1][:, j, :], self.mhi[:])
        sk = sc.sb("sk", [128, 6])
        self.ld(sk[:], self.w["swa_sink"][l].partition_broadcast(128))
        self.act(sk[:], sk[:], AF.Exp)
        es = sc.sb("es", [128, 2, 3, 128])
        self.memset("dve", es[:], 0.0)
        for g in range(2):
            for j in range(3):
                self.ts("dve", es[:, g, j, :], es[:, g, j, :], sk[:, 3 * g + j:3 * g + j + 1], None, op0=ALU.add)
        qb = [sc.sb("qb%d" % i, [128, 3, 512], BF16) for i in range(2)]
        pT = [sc.sb("pTs%d" % i, [128, 384], BF16) for i in range(6)]
        yo = [sc.sb("yos%d" % i, [64, 2, 3, 512], BF16) for i in range(2)]
        rd = [sc.sb("rds%d" % i, [128, 384]) for i in range(2)]
        S = self.pb[0:4] + self.pb[6:8]
        acc = self.pb[4:6]
        chunks = self.chunks[1:] if last else self.chunks

        def ld_q(ci):
            t0, n = chunks[ci]
            for g in range(2):
                self.ld(qb[ci % 2][g * 64:(g + 1) * 64, :, 0:n], self.qsT[3 * g:3 * g + 3, :, t0:t0 + n].rearrange("j d t -> d j t"), W=[qb[ci % 2]])

        items = []
        it = 0
        gi = 0
        for ci, (t0, n) in enumerate(chunks):
            for qt in range(n // 128):
                tile = t0 // 128 + qt
                if t0 < TC:
                    kts = [(0, None), (1, None)]
                else:
                    b = tile - TC // 128
                    kts = [(0, None), (1, None)]
                    if b >= 1:
                        kts.append((tile - 1, 0))
                    kts.append((tile, None))
                    if b + 1 < T // 128:
                        kts.append((tile + 1, 1))
                for g in range(2):
                    for j, (kt, msk) in enumerate(kts):
                        pre = None
                        if j == 0 and qt == 0 and g == 0:
                            def pre(ci=ci):
                                if ci + 1 < len(chunks):
                                    ld_q(ci + 1)

                        def s1(it=it, ci=ci, qt=qt, g=g, kt=kt, msk=msk):
                            P_ = slice(g * 64, (g + 1) * 64)
                            sv = S[it % 6][:, 0:384].rearrange("p (j t) -> p j t", j=3)
                            self.mm(sv, ks[P_, kt * 128:(kt + 1) * 128], qb[ci % 2][P_, :, qt * 128:(qt + 1) * 128], W=[S[it % 6]])
                            self.act(pT[it % 6][:], S[it % 6][:, 0:384], AF.Exp)
                            if msk is not None:
                                self.tt("dve", pT[it % 6][:], pT[it % 6][:], m3[msk][:].rearrange("p j t -> p (j t)"), ALU.mult)

                        def s2(it=it, gi=gi, g=g, kt=kt, first=(j == 0), lastk=(j == len(kts) - 1)):
                            self.mm(acc[gi % 2][:, 0:384], vs[:, g, kt, :], pT[it % 6][:], start=first, stop=lastk)

                        fin = None
                        if j == len(kts) - 1:
                            def fin(gi=gi, g=g, ci=ci, qt=qt, t0=t0, n=n):
                                a, r_, y_ = acc[gi % 2], rd[gi % 2], yo[ci % 2]
                                self.tt("dve", r_[64:128, :], a[64:128, 0:384], es[64:128, g].rearrange("p j t -> p (j t)"), ALU.add)
                                self.cp("act", r_[0:64, :], r_[64:128, :])
                                self.recip(r_[0:64, :], r_[0:64, :])
                                self.tt("dve", y_[:, g, :, qt * 128:(qt + 1) * 128], a[0:64, 0:384].rearrange("p (j t) -> p j t", j=3),
                                        r_[0:64, :].rearrange("p (j t) -> p j t", j=3), ALU.mult, R=[a, r_], W=[y_])
                                if qt == n // 128 - 1 and g == 1:
                                    for gg in range(2):
                                        r0 = 640 + gg * 192
                                        self.ld(self.ycT[r0:r0 + 192, t0:t0 + n].rearrange("(j d) t -> d j t", j=3), y_[:, gg, :, 0:n], W=["ycT"])
                        items.append((pre, s1, s2, fin))
                        it += 1
                    gi += 1
        ld_q(0)
        self.pipeline(items)
        sc.close()

    def ldT2(self, sc, dst, src_rows, n, m, pbank):
        if getattr(self, "_ldT2_sc", None) is not sc:
            self._ldT2_sc = sc
            self._ldT2_stg = [sc.sb("ldT2_%d" % i, [128, 128]) for i in range(2)]
            self._ldT2_n = 0
        self._ldT2_n += 1
        stg = self._ldT2_stg[self._ldT2_n % 2]
        self.ld(stg[0:n, 0:m], src_rows)
        self.tr(pbank[0:m, 0:n], stg[0:n, 0:m], self.identF[0:n, 0:n])
        self.cp("dve", dst, pbank[0:m, 0:n])

    def phaseD(self, l, last):
        nc, T, NT = self.nc, self.T, self.NT
        W = self.w
        sc = Scope(self)
        TPI = 6.28318
        pb = self.pb
        sm = lambda name, shape=(128, 16): sc.sb(name, list(shape))
        aT = sm("aT", (64, 2, 32))
        self.ldT2(sc, aT[:, 0, :], W["s5_a_re"][l].rearrange("d g p -> (d g) p"), 32, 64, pb[0])
        self.ldT2(sc, aT[:, 1, :], W["s5_a_im"][l].rearrange("d g p -> (d g) p"), 32, 64, pb[1])
        are, aim, dtl = sm("are"), sm("aim"), sm("dtl")
        a4 = aT[:].rearrange("p r (dk gl) -> p r dk gl", gl=2)
        for (dst, ri) in ((are, 0), (aim, 1)):
            self.cp("dve", dst[0:64, :], a4[:, ri, :, 0])
            self.cp("act", dst[64:128, :], a4[:, ri, :, 1])
        ldt = sm("ldt", (128, 32))
        self.ld(ldt[:], W["s5_log_dt"][l].rearrange("d g -> (d g)").partition_broadcast(128))
        l3 = ldt[:].rearrange("p (dk gl) -> p dk gl", gl=2)
        self.act(dtl[0:64, :], l3[0:64, :, 0], AF.Exp)
        self.act(dtl[64:128, :], l3[64:128, :, 1], AF.Exp)
        rr, phi, w_, nf, sn0, cs0 = sm("rr"), sm("phi"), sm("w_"), sm("nf"), sm("sn0"), sm("cs0")
        self.tt("dve", rr[:], are[:], dtl[:], ALU.mult)
        self.act(rr[:], rr[:], AF.Exp)
        self.tt("dve", phi[:], aim[:], dtl[:], ALU.mult)
        self.ts("dve", phi[:], phi[:], 1.0 / TWO_PI, None, op0=ALU.mult)
        self.ts("dve", w_[:], phi[:], MAGIC, None, op0=ALU.add)
        self.stt(nf[:], w_[:], -MAGIC, phi[:], ALU.add, ALU.subtract)
        self.act(sn0[:], nf[:], AF.Sin, scale=-TPI)
        self.stt(nf[:], nf[:], -1.0, nf[:], ALU.mult, ALU.max)
        self.act(cs0[:], nf[:], AF.Sin, scale=-TPI, bias=self.cst[:, 3:4])
        nr, ni, den, cre, cim, ncim, t_ = sm("nr"), sm("ni"), sm("den"), sm("cre"), sm("cim"), sm("ncim"), sm("t_")
        self.tt("dve", nr[:], rr[:], cs0[:], ALU.mult)
        self.ts("dve", nr[:], nr[:], -1.0, None, op0=ALU.add)
        self.tt("dve", ni[:], rr[:], sn0[:], ALU.mult)
        self.tt("dve", den[:], are[:], are[:], ALU.mult)
        self.tt("dve", t_[:], aim[:], aim[:], ALU.mult)
        self.tt("dve", den[:], den[:], t_[:], ALU.add)
        self.recip(den[:], den[:])
        self.tt("dve", cre[:], nr[:], are[:], ALU.mult)
        self.tt("dve", t_[:], ni[:], aim[:], ALU.mult)
        self.tt("dve", cre[:], cre[:], t_[:], ALU.add)
        self.tt("dve", cre[:], cre[:], den[:], ALU.mult)
        self.tt("dve", cim[:], ni[:], are[:], ALU.mult)
        self.tt("dve", t_[:], nr[:], aim[:], ALU.mult)
        self.tt("dve", cim[:], cim[:], t_[:], ALU.subtract)
        self.tt("dve", cim[:], cim[:], den[:], ALU.mult)
        self.ts("dve", ncim[:], cim[:], -1.0, None, op0=ALU.mult)
        phh, phl = sm("phh"), sm("phl")
        self.ts("dve", t_[:], phi[:], 1024.0, MAGIC, op0=ALU.mult, op1=ALU.add)
        self.ts("dve", phh[:], t_[:], -MAGIC, 1.0 / 1024.0, op0=ALU.add, op1=ALU.mult)
        self.tt("dve", phl[:], phi[:], phh[:], ALU.subtract)
        if "dbgP" in self.dbg:
            dP = self.nc.dram_tensor("dbgP", [128, 16, 16], F32, kind="ExternalOutput").ap()
            for i, t in enumerate((are, aim, dtl, rr, phi, sn0, cs0, cre, cim, phh, phl)):
                self.ld(dP[:, i, :], t[:], W=["dbgP"])
        Bri = sc.sb("Bri", [128, 2, 16, 16])
        for ri, nm in enumerate(("s5_b_re", "s5_b_im")):
            for d in range(2):
                for gl in range(2):
                    self.ld(Bri[gl * 64:(gl + 1) * 64, ri, d * 8:(d + 1) * 8, :],
                            W[nm][l, d].rearrange("(k gl) p h -> gl p k h", gl=2)[gl], W=[Bri])
        bb = sc.sb("bb", [128, 2, 16, 16])
        for dk in range(16):
            c1, c2, c3 = cre[:, dk:dk + 1], cim[:, dk:dk + 1], ncim[:, dk:dk + 1]
            self.ts("dve", bb[:, 0, dk, :], Bri[:, 0, dk, :], c1, None, op0=ALU.mult)
            self.stt(bb[:, 0, dk, :], Bri[:, 1, dk, :], c3, bb[:, 0, dk, :], ALU.mult, ALU.add)
            self.ts("dve", bb[:, 1, dk, :], Bri[:, 0, dk, :], c2, None, op0=ALU.mult)
            self.stt(bb[:, 1, dk, :], Bri[:, 1, dk, :], c1, bb[:, 1, dk, :], ALU.mult, ALU.add)
        BbT = sc.sb("BbT", [128, 32, 128], BF16)
        Et = [sc.sb("Et%d" % i, [128, 64]) for i in range(2)]
        for e_ in Et:
            self.memset("dve", e_[:], 0.0)
        for dk in range(16):
            k = dk % 8
            hi = (k % 4 == 3)
            p0 = 64 if hi else (32 * k) % 128
            nr_ = 64 if hi else 32
            c0 = 32 if hi else 0
            for ri in range(2):
                e_ = Et[ri]
                self.cp("dve", e_[0:64, c0:c0 + 16], bb[0:64, ri, dk, :])
                self.cp("dve", e_[64:128, c0 + 16:c0 + 32], bb[64:128, ri, dk, :])
                pbk = pb[2 + ri]
                self.tr(pbk[0:nr_, 0:128], e_[:, 0:nr_], self.identF[:])
                self.cp("act", BbT[p0:p0 + nr_, dk * 2 + ri, :], pbk[0:nr_, 0:128])
                if c0 == 0 and k % 4 == 2:
                    pass
                if hi or True:
                    self.memset("dve", e_[:, c0:c0 + 32], 0.0)
        CT = sc.sb("CT", [64, 2, 512])
        for ri, nm in enumerate(("s5_c_re", "s5_c_im")):
            rows = W[nm][l].rearrange("d g h p -> (d g h) p")
            for blk in range(4):
                self.ldT2(sc, CT[:, ri, blk * 128:(blk + 1) * 128], rows[blk * 128:(blk + 1) * 128, :], 128, 64, pb[4 + blk % 2])
        Cm = sc.sb("Cm", [128, 48, 128], BF16)
        self.memset("dve", Cm[:], 0.0)
        for dk in range(16):
            d, k = dk // 8, dk % 8
            for pl, (ri, sg_) in enumerate(((0, 1.0), (1, -1.0), (0, -1.0))):
                for gl in range(2):
                    g = 2 * k + gl
                    c0 = (k % 4) * 32 + gl * 16
                    src = CT[0:64, ri, (d * 16 + g) * 16:(d * 16 + g) * 16 + 16]
                    self.act(Cm[gl * 64:(gl + 1) * 64, dk * 3 + pl, c0:c0 + 16], src, AF.Copy, scale=sg_)
        dsk, bgl = sm("dsk", (128, 2)), sm("bgl", (128, 2))
        self.ldT2(sc, dsk[:], W["s5_d"][l].rearrange("(c p) -> c p", p=128), 2, 128, pb[6])
        self.ldT2(sc, bgl[:], W["s5_b_glu"][l].rearrange("(c p) -> c p", p=128), 2, 128, pb[7])
        wgl = sc.sb("wgl", [128, 2, 256], BF16)
        self.ld(wgl[:], W["s5_w_glu"][l].rearrange("(k p) n -> p k n", p=128), e="pool")
        ji = sc.sb("ji", [128, 512], I32)
        jrow = sc.sb("jrow", [128, 512])
        self.fw.op("pool", lambda: nc.gpsimd.iota(ji[:], pattern=[[1, 512]], base=0, channel_multiplier=0), [], [ji])
        self.cp("dve", jrow[:], ji[:])
        J = sc.sb("J", [128, 8, 512])
        rmul = sc.sb("rmul", [128, 8, 512])
        tq = [sc.sb("tq%d" % i, [128, 512]) for i in range(2)]

        def build_tables(d):
            for k in range(8):
                dk = d * 8 + k
                v, w2 = tq
                self.ts("dve", v[:], jrow[:], phh[:, dk:dk + 1], None, op0=ALU.mult)
                self.ts("dve", w2[:], v[:], MAGIC, None, op0=ALU.add)
                self.stt(w2[:], w2[:], -MAGIC, v[:], ALU.add, ALU.subtract)
                self.stt(J[:, k, :], jrow[:], phl[:, dk:dk + 1], w2[:], ALU.mult, ALU.subtract)
                self.ts("pool", rmul[:, k, :], jrow[:], 0.0, rr[:, dk:dk + 1], op0=ALU.mult, op1=ALU.add)
        ub = sc.sb("ub", [128, 2, NT], BF16)
        self.ld(ub[:], self.uTb.rearrange("(c p) t -> p c t", p=128))
        R2 = lambda name, dt=F32: [sc.sb("%s%d" % (name, i), [128, 512], dt) for i in range(2)]
        R1 = lambda name: [sc.sb(name, [128, 512])] * 2
        cs_, sn_, w1_, nf_ = R2("cs"), R2("sn"), R1("w1"), R2("nfm")
        bre_, bim_, zr_, zi_, t1_, t2_ = R2("bre"), R2("bim"), R1("zr"), R1("zi"), R1("t1"), R1("t2")
        t3_, t4_ = R1("t3"), R1("t4")
        zre_, zim_, u1_ = R1("zre"), R1("zim"), R2("u1")
        hh = [sc.sb("hh%d" % k, [128, 4, 512], BF16) for k in range(8)]
        carry = sc.sb("carry", [128, 2, 8, 2])
        bs = [sc.sb("bs%d" % i, [128, 5, 8]) for i in range(2)]
        u32 = [sc.sb("u32_%d" % i, [128, 2, 512]) for i in range(2)]
        ys = [sc.sb("ys%d" % i, [128, 2, 512]) for i in range(2)]
        zt = [sc.sb("zt", [128, 2, 512])] * 2
        zb = [sc.sb("zb", [128, 2, 512], BF16)] * 2
        yo = [sc.sb("yod", [128, 2, 512], BF16)] * 2
        nlat = T // 512
        units = []
        chunk_ctx = {}
        uidx = {}

        def finish_chunk(d, oi):
            t0, n, b_ = chunk_ctx[(d, oi)]
            for ct in range(2):
                acc = pb[4 + ct]
                lst = [(k, pl) for k in range(4 * ct, 4 * ct + 4) for pl in range(4)]
                for i, (k, pl) in enumerate(lst):
                    cpl = (0, 2, 1, 1)[pl]
                    self.mm(acc[:, 0:n], Cm[:, (d * 8 + k) * 3 + cpl, :], hh[k][:, pl, 0:n], start=(i == 0), stop=(i == len(lst) - 1))
                if d == 0:
                    self.stt(ys[oi % 2][:, ct, 0:n], u32[oi % 2][:, ct, 0:n], dsk[:, ct:ct + 1], acc[:, 0:n], ALU.mult, ALU.add)
                else:
                    self.tt("dve", ys[oi % 2][:, ct, 0:n], ys[oi % 2][:, ct, 0:n], acc[:, 0:n], ALU.add)
            if d == 0:
                self.ld(self.ysum[:, t0:t0 + n].rearrange("(c p) t -> p c t", p=128), ys[oi % 2][:, :, 0:n], W=["ysum"])
            else:
                y_, z_, zb_ = ys[oi % 2], zt[oi % 2], zb[oi % 2]
                for ct in range(2):
                    y1, z1 = y_[:, ct, 0:n], z_[:, ct, 0:n]
                    self.tt("pool", z1, y1, y1, ALU.mult)
                    self.ts("dve", z1, z1, 0.044715, 1.0, op0=ALU.mult, op1=ALU.add)
                    self.tt("pool", z1, z1, y1, ALU.mult)
                    self.act(z1, z1, AF.Tanh, scale=math.sqrt(2.0 / math.pi))
                    self.stt(z1, z1, 1.0, y1, ALU.add, ALU.mult)
                    self.ts("dve", z1, z1, 0.5, None, op0=ALU.mult)
                    self.cp("act", zb_[:, ct, 0:n], z1)
                for ct in range(2):
                    pg = pb[6 + ct]
                    for kk in range(2):
                        self.mm(pg[:, 0:n], wgl[:, kk, ct * 128:(ct + 1) * 128], zb_[:, kk, 0:n], start=(kk == 0), stop=(kk == 1))
                    gt = u1_[ct]
                    self.act(gt[:, 0:n], pg[:, 0:n], AF.Sigmoid, bias=bgl[:, ct:ct + 1])
                    self.tt("dve", yo[oi % 2][:, ct, 0:n], z_[:, ct, 0:n], gt[:, 0:n], ALU.mult)
                self.ld(self.ycT[384:640, t0:t0 + n].rearrange("(c p) t -> p c t", p=128), yo[oi % 2][:, :, 0:n], W=["ycT"])

        def emit_T(u):
            d, oi, k = u
            t0, n, b_ = chunk_ctx[(d, oi)]
            sgn = 1.0 if d == 0 else -1.0
            dk = d * 8 + k
            i2 = uidx.setdefault(u, len(uidx)) % 2
            cs, sn, w1, nfm = cs_[i2], sn_[i2], w1_[i2], nf_[i2]
            base, b2p, nbase = b_[:, 2, k:k + 1], b_[:, 3, k:k + 1], b_[:, 4, k:k + 1]
            hi = (k % 4 == 3)
            p0 = 64 if hi else (32 * k) % 128
            p1 = p0 + (64 if hi else 32)
            c_ = (32 * k) // 128
            un = uidx[u]
            pre_, pim_ = pb[(2 * un) % 4], pb[(2 * un + 1) % 4]
            return [
                lambda: self.ts("dve", w1[:, 0:n], J[:, k, 0:n], base, MAGIC, op0=ALU.add, op1=ALU.add),
                lambda: self.mm(pre_[:, 0:n], BbT[p0:p1, dk * 2, :], ub[p0:p1, c_, t0:t0 + n]),
                lambda: self.stt(nfm[:, 0:n], w1[:, 0:n], -MAGIC, J[:, k, 0:n], ALU.add, ALU.subtract),
                lambda: self.mm(pim_[:, 0:n], BbT[p0:p1, dk * 2 + 1, :], ub[p0:p1, c_, t0:t0 + n]),
                lambda: self.act(sn[:, 0:n], nfm[:, 0:n], AF.Sin, scale=-sgn * TPI, bias=b2p),
                lambda: self.cp("act", bre_[i2][:, 0:n], pre_[:, 0:n]),
                lambda: self.act(w1[:, 0:n], nfm[:, 0:n], AF.Abs, bias=nbase),
                lambda: self.cp("act", bim_[i2][:, 0:n], pim_[:, 0:n]),
                lambda: self.act(cs[:, 0:n], w1[:, 0:n], AF.Sin, scale=-TPI, bias=self.cst[:, 3:4]),
            ]

        def emit_R(u):
            d, oi, k = u
            t0, n, b_ = chunk_ctx[(d, oi)]
            i2 = uidx[u] % 2
            cs, sn, bre, bim = cs_[i2], sn_[i2], bre_[i2], bim_[i2]
            zr, zi, t1, t2, t3, t4 = zr_[0], zi_[0], t1_[0], t2_[0], t3_[0], t4_[0]
            zre, zim = zre_[0], zim_[0]
            rv = (lambda a: a[:, 0:n]) if d == 0 else (lambda a: a[:, 0:n][:, ::-1])
            H = hh[k]
            un = uidx[u]
            pre_, pim_ = pb[(2 * un) % 4], pb[(2 * un + 1) % 4]

            def sc_(zin, zout, ri):
                init = 0.0 if oi == 0 else carry[:, d, k, ri:ri + 1]
                self.scan(rv(zout), rmul[:, k, 0:n], rv(zin), init, R=[rmul, zin, carry], W=[zout])

            def cy_(zout, ri):
                lastc = zout[:, n - 1:n] if d == 0 else zout[:, 0:1]
                self.cp("dve", carry[:, d, k, ri:ri + 1], lastc)

            ops = [
                lambda: self.tt("dve", t1[:, 0:n], pre_[:, 0:n], cs[:, 0:n], ALU.mult),
                lambda: self.tt("pool", t3[:, 0:n], bim[:, 0:n], cs[:, 0:n], ALU.mult),
                lambda: self.tt("dve", t2[:, 0:n], pim_[:, 0:n], sn[:, 0:n], ALU.mult),
                lambda: self.tt("pool", t4[:, 0:n], bre[:, 0:n], sn[:, 0:n], ALU.mult),
                lambda: self.tt("dve", zr[:, 0:n], t1[:, 0:n], t2[:, 0:n], ALU.add),
                lambda: self.tt("pool", zi[:, 0:n], t3[:, 0:n], t4[:, 0:n], ALU.subtract),
                lambda: sc_(zr, zre, 0),
                lambda: cy_(zre, 0),
                lambda: sc_(zi, zim, 1),
                lambda: cy_(zim, 1),
                lambda: self.tt("dve", H[:, 0, 0:n], zre[:, 0:n], cs[:, 0:n], ALU.mult),
                lambda: self.tt("pool", H[:, 2, 0:n], zre[:, 0:n], sn[:, 0:n], ALU.mult),
                lambda: self.tt("dve", H[:, 1, 0:n], zim[:, 0:n], sn[:, 0:n], ALU.mult),
                lambda: self.tt("pool", H[:, 3, 0:n], zim[:, 0:n], cs[:, 0:n], ALU.mult),
            ]
            if k == 7:
                ops.append(lambda: finish_chunk(d, oi))
            return ops

        def run_merged(a, b):
            i = j = 0
            while i < len(a) or j < len(b):
                if i < len(a):
                    a[i]()
                    i += 1
                if j < len(b):
                    b[j]()
                    j += 1

        for d in range(2):
            sgn = 1.0 if d == 0 else -1.0
            build_tables(d)
            order = list(range(len(self.chunks))) if d == 0 else [0] + list(range(len(self.chunks) - 1, 0, -1))
            for oi, ci in enumerate(order):
                t0, n = self.chunks[ci]
                rho0 = float(t0) if d == 0 else (float(T) if ci == 0 else float(t0 - TC))
                b_ = bs[oi % 2]
                dsl = slice(d * 8, d * 8 + 8)
                self.ts("dve", b_[:, 0, :], phh[:, dsl], rho0, None, op0=ALU.mult)
                self.ts("dve", b_[:, 1, :], b_[:, 0, :], MAGIC, None, op0=ALU.add)
                self.stt(b_[:, 1, :], b_[:, 1, :], -MAGIC, b_[:, 0, :], ALU.add, ALU.subtract)
                self.stt(b_[:, 2, :], phl[:, dsl], rho0, b_[:, 1, :], ALU.mult, ALU.subtract)
                self.ts("dve", b_[:, 3, :], b_[:, 2, :], sgn * TPI, None, op0=ALU.mult)
                self.ts("dve", b_[:, 4, :], b_[:, 2, :], -1.0, None, op0=ALU.mult)
                if d == 0:
                    self.ld(u32[oi % 2][:, :, 0:n], self.uT32[:, t0:t0 + n].rearrange("(c p) t -> p c t", p=128))
                else:
                    self.ld(ys[oi % 2][:, :, 0:n], self.ysum[:, t0:t0 + n].rearrange("(c p) t -> p c t", p=128))
                chunk_ctx[(d, oi)] = (t0, n, b_)
                for k in range(8):
                    units.append((d, oi, k))
                    tl = emit_T(units[-1])
                    rl = emit_R(units[-2]) if len(units) >= 2 else []
                    run_merged(rl, tl)
                    if k == 7 and oi == len(order) - 1:
                        run_merged(emit_R(units[-1]), [])
                        units.clear()
                if "dbgD" in self.dbg:
                    pass
        sc.close()

    def phaseE(self, l, xin, xout, last):
        nc, T, NT = self.nc, self.T, self.NT
        W = self.w
        pb = self.pb
        sc = Scope(self)
        wout = sc.sb("wout", [128, 8, 1024], BF16)
        for k in range(8):
            self.ld(wout[:, k, :], W["w_out"][l][k * 128:(k + 1) * 128, :], e="pool", W=[wout])
        yc = [sc.sb("yc%d" % i, [128, 8, 512], BF16) for i in range(2)]
        xs = [sc.sb("xe%d" % i, [128, 8, 512]) for i in range(2)]
        xo = [sc.sb("xo%d" % i, [128, 8, 512]) for i in range(2)]
        chunks = self.chunks[1:] if last else self.chunks

        def ld1(ci):
            t0, n = chunks[ci]
            self.ld(yc[ci % 2][:, :, 0:n], self.ycT[:, t0:t0 + n].rearrange("(k p) t -> p k t", p=128))
            self.ld(xs[ci % 2][:, :, 0:n], xin[:, t0:t0 + n].rearrange("(k p) t -> p k t", p=128))

        ld1(0)
        nb = 0
        for ci, (t0, n) in enumerate(chunks):
            col_ = 1 if t0 < TC else 0
            if ci + 1 < len(chunks):
                ld1(ci + 1)
            for m in range(8):
                p = pb[nb % 8]
                nb += 1
                for k in range(8):
                    self.mm(p[:, 0:n], wout[:, k, m * 128:(m + 1) * 128], yc[ci % 2][:, k, 0:n], start=(k == 0), stop=(k == 7))
                self.stt(xo[ci % 2][:, m, 0:n], p[:, 0:n], self.mod[:, 16 + m, col_:col_ + 1], xs[ci % 2][:, m, 0:n], ALU.mult, ALU.add)
            self.ld(xout[:, t0:t0 + n].rearrange("(k p) t -> p k t", p=128), xo[ci % 2][:, :, 0:n], W=[xout.tensor.name])
        sc.close()
        sc = Scope(self)
        wup = sc.sb("wup", [128, 8, 2 * DFF], BF16)
        for k in range(8):
            for q in range(4):
                self.ld(wup[:, k, q * 1408:(q + 1) * 1408], W["w_up"][l][k * 128:(k + 1) * 128, q * 1408:(q + 1) * 1408], e="pool", W=[wup])
        wdn = sc.sb("wdn", [128, 22, 1024], BF16)
        for f in range(22):
            self.ld(wdn[:, f, :], W["w_down"][l][f * 128:(f + 1) * 128, :], e="pool", W=[wdn])
        cw = sc.sb("cw", [128, 132])
        cwr = W["conv_w"][l].rearrange("j (f p) -> (j f) p", p=128)
        self.ldT2(sc, cw[:, 0:128], cwr[0:128, :], 128, 128, pb[0])
        self.ldT2(sc, cw[:, 128:132], cwr[128:132, :], 4, 128, pb[1])
        cb = sc.sb("cb", [128, 44])
        self.ldT2(sc, cb[:], W["conv_b"][l].rearrange("(f p) -> f p", p=128), 44, 128, pb[2])
        WN = 256
        xw = [sc.sb("xw%d" % i, [128, 8, WN]) for i in range(2)]
        sq = sc.sb("sqe", [128, 8, WN], BF16)
        rs = sc.sb("rse", [128, WN])
        tmp = [sc.sb("tme%d" % i, [128, WN]) for i in range(2)]
        h2 = sc.sb("h2", [128, 8, WN], BF16)
        actT = sc.sb("actT", [128, 22, WN], BF16)
        ta = [sc.sb("ta%d" % i, [128, WN]) for i in range(2)]
        tg = [sc.sb("tg%d" % i, [128, WN]) for i in range(2)]
        sg = [sc.sb("sg%d" % i, [128, WN]) for i in range(2)]
        xo2 = sc.sb("xo2", [128, 8, WN])
        wins = []
        seqs = ([] if last else [(0, TC, 1)]) + [(TC, T, 0)]
        for (off, Ls, col_) in seqs:
            for s0 in range(0, Ls, 254):
                wins.append((off, Ls, col_, s0, min(254, Ls - s0)))

        def ldw(wi):
            off, Ls, col_, s0, nv = wins[wi]
            lo, hi = (s0 == 0), (s0 + nv == Ls)
            c0, c1 = (1 if lo else 0), (nv + 1 if hi else nv + 2)
            self.ld(xw[wi % 2][:, :, c0:c1], xout[:, off + s0 - 1 + c0:off + s0 - 1 + c1].rearrange("(k p) t -> p k t", p=128))

        ldw(0)
        nb = 0
        for wi, (off, Ls, col_, s0, nv) in enumerate(wins):
            lo, hi = (s0 == 0), (s0 + nv == Ls)
            c0, c1 = (1 if lo else 0), (nv + 1 if hi else nv + 2)
            nw = nv + 2
            x_ = xw[wi % 2]
            if wi + 1 < len(wins):
                ldw(wi + 1)
            for k in range(8):
                self.act(sq[:, k, c0:c1], x_[:, k, c0:c1], AF.Square)
            pst = pb[6 + wi % 2]
            for k in range(8):
                self.mm(pst[:, c0:c1], self.onesB[:], sq[:, k, c0:c1], start=(k == 0), stop=(k == 7))
            self.rstd(rs[:, c0:c1], pst[:, c0:c1], 1.0 / D, self.cst[:, 0:1])
            if lo:
                self.memset("pool", h2[:, :, 0:1], 0.0)
            if hi:
                self.memset("pool", h2[:, :, nv + 1:nv + 2], 0.0)
            for k in range(8):
                t_ = tmp[k % 2]
                self.stt(t_[:, c0:c1], x_[:, k, c0:c1], self.AB[:, 2, k, col_:col_ + 1], rs[:, c0:c1], ALU.mult, ALU.mult)
                self.act(h2[:, k, c0:c1], t_[:, c0:c1], AF.Identity, bias=self.AB[:, 3, k, col_:col_ + 1])
            for f in range(22):
                pa, pg = pb[(2 * nb) % 6], pb[(2 * nb + 1) % 6]
                a_, g_, s_ = ta[nb % 2], tg[nb % 2], sg[nb % 2]
                nb += 1
                for k in range(8):
                    self.mm(pa[:, 0:nw], wup[:, k, f * 128:(f + 1) * 128], h2[:, k, 0:nw], start=(k == 0), stop=(k == 7))
                for k in range(8):
                    self.mm(pg[:, 0:nw], wup[:, k, DFF + f * 128:DFF + (f + 1) * 128], h2[:, k, 0:nw], start=(k == 0), stop=(k == 7))
                for (pp, tt_, ft) in ((pa, a_, f), (pg, g_, 22 + f)):
                    w0, w1, w2 = cw[:, ft:ft + 1], cw[:, 44 + ft:45 + ft], cw[:, 88 + ft:89 + ft]
                    self.act(tt_[:, 0:nv], pp[:, 1:nv + 1], AF.Identity, scale=w1, bias=cb[:, ft:ft + 1])
                    self.stt(tt_[:, 0:nv], pp[:, 0:nv], w0, tt_[:, 0:nv], ALU.mult, ALU.add)
                    self.stt(tt_[:, 0:nv], pp[:, 2:nv + 2], w2, tt_[:, 0:nv], ALU.mult, ALU.add)
                self.act(s_[:, 0:nv], g_[:, 0:nv], AF.Silu)
                self.tt("pool", actT[:, f, 0:nv], s_[:, 0:nv], a_[:, 0:nv], ALU.mult)
            for m in range(8):
                pd = pb[6 + m % 2]
                for f in range(22):
                    self.mm(pd[:, 0:nv], wdn[:, f, m * 128:(m + 1) * 128], actT[:, f, 0:nv], start=(f == 0), stop=(f == 21))
                self.stt(xo2[:, m, 0:nv], pd[:, 0:nv], self.mod[:, 40 + m, col_:col_ + 1], x_[:, m, 1:nv + 1], ALU.mult, ALU.add)
            self.ld(xin[:, off + s0:off + s0 + nv].rearrange("(k p) t -> p k t", p=128), xo2[:, :, 0:nv], W=[xin.tensor.name])
        sc.close()

    def phaseOut(self, xT):
        T = self.T
        sc = Scope(self)
        xi = [sc.sb("xoi%d" % i, [128, 8, 128]) for i in range(2)]
        og = [sc.sb("ostg%d" % i, [128, D]) for i in range(2)]
        for i in range(T // 128):
            t0 = TC + i * 128
            self.ld(xi[i % 2][:], xT[:, t0:t0 + 128].rearrange("(k p) t -> p k t", p=128))
            for half in range(2):
                p = self.pb[(2 * i + half) % 8]
                for q in range(4):
                    self.tr(p[:, q * 128:(q + 1) * 128], xi[i % 2][:, half * 4 + q, :], self.identF[:])
                self.cp("act" if half else "dve", og[i % 2][:, half * 512:(half + 1) * 512], p[:])
            self.ld(self.out_d[i * 128:(i + 1) * 128, :], og[i % 2][:], W=["out"])
        sc.close()


def rope_ab():
    ab = np.zeros((128, 4), np.float32)
    inv8 = 10000.0 ** (-np.arange(8) / 8.0)
    inv16 = 10000.0 ** (-np.arange(16) / 16.0)
    for i in range(32):
        j = i % 16
        ab[64 + i, 0 if j < 8 else 1] = inv8[j % 8] / TWO_PI
    for p in range(128):
        j = (p % 64) % 32
        ab[p, 2 if j < 16 else 3] = inv16[j % 16] / TWO_PI
    return ab


WEIGHT_KEYS = ["w_mod", "b_mod", "norm1", "w_in", "mla_q_lora_g", "mla_w_uq", "mla_kv_lora_g", "mla_w_ukv", "mla_q_norm",
               "mla_k_norm", "s5_a_re", "s5_a_im", "s5_log_dt", "s5_b_re", "s5_b_im", "s5_c_re", "s5_c_im", "s5_d",
               "s5_w_glu", "s5_b_glu", "swa_q_norm", "swa_k_norm", "swa_sink", "w_out", "norm2", "w_up", "conv_w",
               "conv_b", "w_down"]


def make_in_maps(inputs, n_cores, T):
    maps = []
    ab = rope_ab()
    for b in range(n_cores):
        m = {k: np.ascontiguousarray(inputs[k], dtype=np.float32) for k in WEIGHT_KEYS}
        m["x"] = np.ascontiguousarray(inputs["x"][b, :T], dtype=np.float32)
        m["ctx"] = np.ascontiguousarray(inputs["ctx"][b], dtype=np.float32)
        m["cvec"] = np.ascontiguousarray(np.stack([inputs["c"][b], inputs["c_ctx"]]), dtype=np.float32)
        m["rope_ab"] = ab
        maps.append(m)
    return maps


def kernel(**inputs):
    B, T = inputs["x"].shape[0], inputs["x"].shape[1]
    kb = KB(T, 4)
    nc = kb.build()
    maps = make_in_maps(inputs, B, T)
    res = run_bass_kernel_spmd(nc, maps, core_ids=list(range(B)))
    return np.stack([np.asarray(r["out"], dtype=np.float32) for r in res.results], axis=0)
```
